# Optimizing a Trainium2 kernel written in Bass

```python
import math
import jax, jax.numpy as jnp
from jax import lax
import numpy as np

D_MODEL = 1024
BATCH = 4
SEQ = 4096
DEPTH = 1
DEC_BATCH = 32
DEC_SEQ = 64
PAST_LEN = 4096

CHUNK = 64
N_HEADS = 16
KV_HEADS = 4
HEAD_DIM = 64
GROUP = N_HEADS // KV_HEADS
WINDOW = 128
N_WIN_CHUNKS = WINDOW // CHUNK
BAND = WINDOW + CHUNK
CONV_W = 31
D_CONV = D_MODEL
D_FF = ((8 * D_MODEL // 3 + 255) // 256) * 256
NUM_BUCKETS = 32
MAX_DISTANCE = 128
EPS = 1e-6
NEG = -1e30
Q_W = N_HEADS * HEAD_DIM
KV_W = KV_HEADS * HEAD_DIM
IN_W = Q_W + 2 * KV_W + 2 * D_CONV + 2 * D_MODEL

kernel_name = "chunk_causal_hybrid_swa_conformer_step"


def _rmsnorm(x, g):
    x32 = x.astype(jnp.float32)
    y = x32 * lax.rsqrt(jnp.mean(x32 * x32, axis=-1, keepdims=True) + EPS)
    return (y * g.astype(jnp.float32)).astype(x.dtype)


def _layernorm(x, g, b):
    x32 = x.astype(jnp.float32)
    mu = jnp.mean(x32, axis=-1, keepdims=True)
    var = jnp.mean(jnp.square(x32 - mu), axis=-1, keepdims=True)
    y = (x32 - mu) * lax.rsqrt(var + EPS)
    return (y * g.astype(jnp.float32) + b.astype(jnp.float32)).astype(x.dtype)


def _rel_bucket(rel):
    nb = NUM_BUCKETS // 2
    max_exact = nb // 2
    ret = (rel > 0).astype(jnp.int32) * nb
    n = jnp.abs(rel)
    nf = jnp.maximum(n, 1).astype(jnp.float32)
    large = max_exact + (jnp.log(nf / max_exact) / math.log(MAX_DISTANCE / max_exact)
                         * (nb - max_exact)).astype(jnp.int32)
    large = jnp.minimum(large, nb - 1)
    return ret + jnp.where(n < max_exact, n, large)


def _grouped_attention(qb, kb, vb, rel, valid, rel_table, sink):
    s = jnp.einsum('bnqkgd,bnjkd->bnkgqj', qb, kb,
                   preferred_element_type=jnp.float32) * (HEAD_DIM ** -0.5)
    bias = rel_table[_rel_bucket(rel)]
    bias = jnp.transpose(bias, (2, 0, 1)).reshape(KV_HEADS, GROUP, rel.shape[0], rel.shape[1])
    s = s + bias.astype(jnp.float32)
    s = jnp.where(valid[None, :, None, None, None, :], s, NEG)
    sk = sink.astype(jnp.float32).reshape(KV_HEADS, GROUP)[None, None, :, :, None, None]
    m = jnp.maximum(jnp.max(s, axis=-1, keepdims=True), sk)
    p = jnp.exp(s - m)
    p = p / (jnp.sum(p, axis=-1, keepdims=True) + jnp.exp(sk - m))
    o = jnp.einsum('bnkgqj,bnjkd->bnqkgd', p.astype(vb.dtype), vb)
    b, n, q = o.shape[:3]
    return o.reshape(b, n * q, Q_W)


def _swa_prompt(q, k, v, rel_table, sink):
    b, t = q.shape[:2]
    nc = t // CHUNK
    pad = jnp.zeros((b, WINDOW, KV_HEADS, HEAD_DIM), k.dtype)
    kc = jnp.concatenate([pad, k], axis=1).reshape(b, nc + N_WIN_CHUNKS, CHUNK, KV_HEADS, HEAD_DIM)
    vc = jnp.concatenate([pad.astype(v.dtype), v], axis=1).reshape(b, nc + N_WIN_CHUNKS, CHUNK, KV_HEADS, HEAD_DIM)
    kb = jnp.concatenate([kc[:, i:i + nc] for i in range(N_WIN_CHUNKS + 1)], axis=2)
    vb = jnp.concatenate([vc[:, i:i + nc] for i in range(N_WIN_CHUNKS + 1)], axis=2)
    qb = q.reshape(b, nc, CHUNK, KV_HEADS, GROUP, HEAD_DIM)
    kj = jnp.arange(BAND, dtype=jnp.int32)
    rel = kj[None, :] - WINDOW - jnp.arange(CHUNK, dtype=jnp.int32)[:, None]
    valid = (jnp.arange(nc, dtype=jnp.int32)[:, None] * CHUNK - WINDOW + kj[None, :]) >= 0
    return _grouped_attention(qb, kb, vb, rel, valid, rel_table, sink)


def _swa_sample(q, k_all, v_all, rel_table, sink):
    b, ds = q.shape[:2]
    nk = k_all.shape[1]
    qb = q.reshape(b, 1, ds, KV_HEADS, GROUP, HEAD_DIM)
    rel = jnp.arange(nk, dtype=jnp.int32)[None, :] - WINDOW - jnp.arange(ds, dtype=jnp.int32)[:, None]
    valid = jnp.ones((1, nk), dtype=bool)
    return _grouped_attention(qb, k_all[:, None], v_all[:, None], rel, valid, rel_table, sink)


def _conv_module(glu_in, prev, dw_w, dw_b, ln_g, ln_b, w_conv_out):
    a, g = jnp.split(glu_in, 2, axis=-1)
    z = a * jax.nn.sigmoid(g)
    zp = jnp.concatenate([prev.astype(z.dtype), z], axis=1)
    y = lax.conv_general_dilated(zp, dw_w[:, None, :].astype(z.dtype), (1,), 'VALID',
                                 dimension_numbers=('NWC', 'WIO', 'NWC'),
                                 feature_group_count=D_CONV) + dw_b
    y = _layernorm(y, ln_g, ln_b)
    y = y * jax.nn.sigmoid(y)
    return y @ w_conv_out, zp[:, -(CONV_W - 1):]


def _trunk_layer(x, c, k_cache, v_cache, conv_cache, rel_table, w_ada, b_ada, norm1_g, norm2_g,
                 w_in, sink, w_attn_out, dw_w, dw_b, conv_ln_g, conv_ln_b, w_conv_out, w_out,
                 w_ffn_up, w_ffn_down):
    b = x.shape[0]
    mod = jax.nn.silu(c) @ w_ada + b_ada
    sh1, sc1, gt1, sh2, sc2, gt2 = [m[:, None, :] for m in jnp.split(mod, 6, axis=-1)]

    h = _rmsnorm(x, norm1_g) * (1 + sc1) + sh1
    proj = h @ w_in
    o1 = Q_W
    o2 = o1 + KV_W
    o3 = o2 + KV_W
    o4 = o3 + 2 * D_CONV
    q = proj[..., :o1]
    k = proj[..., o1:o2].reshape(b, -1, KV_HEADS, HEAD_DIM)
    v = proj[..., o2:o3].reshape(b, -1, KV_HEADS, HEAD_DIM)
    glu_in = proj[..., o3:o4]
    g_a, g_b = jnp.split(jax.nn.sigmoid(proj[..., o4:]), 2, axis=-1)

    if k_cache is None:
        att = _swa_prompt(q, k, v, rel_table, sink)
        new_k = k[:, -WINDOW:]
        new_v = v[:, -WINDOW:]
        prev = jnp.zeros((b, CONV_W - 1, D_CONV), x.dtype)
    else:
        k_all = jnp.concatenate([k_cache.astype(k.dtype), k], axis=1)
        v_all = jnp.concatenate([v_cache.astype(v.dtype), v], axis=1)
        att = _swa_sample(q, k_all, v_all, rel_table, sink)
        new_k = k_all[:, -WINDOW:]
        new_v = v_all[:, -WINDOW:]
        prev = conv_cache
    y_a = att @ w_attn_out
    y_b, new_conv = _conv_module(glu_in, prev, dw_w, dw_b, conv_ln_g, conv_ln_b, w_conv_out)
    x = x + gt1 * ((g_a * y_a + g_b * y_b) @ w_out)

    h2 = _rmsnorm(x, norm2_g) * (1 + sc2) + sh2
    gate, up = jnp.split(h2 @ w_ffn_up, 2, axis=-1)
    x = x + gt2 * ((jax.nn.silu(gate) * up) @ w_ffn_down)
    return x, new_k, new_v, new_conv


def setup_inputs(seed: int = 0) -> dict:
    key = jax.random.key(seed)
    ks = jax.random.split(key, 32)
    f32 = jnp.float32
    nrm = lambda k, s, sc: jax.random.normal(k, s, f32) * sc
    return {
        "x_prompt": nrm(ks[0], (BATCH, SEQ, D_MODEL), 1.0),
        "x_sample": nrm(ks[1], (DEC_BATCH, DEC_SEQ, D_MODEL), 1.0),
        "cache_k": nrm(ks[2], (DEPTH, DEC_BATCH, WINDOW, KV_HEADS, HEAD_DIM), 1.0),
        "cache_v": nrm(ks[3], (DEPTH, DEC_BATCH, WINDOW, KV_HEADS, HEAD_DIM), 1.0),
        "cache_conv": nrm(ks[4], (DEPTH, DEC_BATCH, CONV_W - 1, D_CONV), 0.5),
        "c_prompt": nrm(ks[5], (BATCH, D_MODEL), 1.0),
        "c_sample": nrm(ks[6], (DEC_BATCH, D_MODEL), 1.0),
        "rel_table": nrm(ks[7], (NUM_BUCKETS, N_HEADS), 0.5),
        "w_ada": nrm(ks[8], (DEPTH, D_MODEL, 6 * D_MODEL), 0.5 * D_MODEL ** -0.5),
        "b_ada": nrm(ks[9], (DEPTH, 6 * D_MODEL), 0.01),
        "norm1_g": 1.0 + nrm(ks[10], (DEPTH, D_MODEL), 0.02),
        "norm2_g": 1.0 + nrm(ks[11], (DEPTH, D_MODEL), 0.02),
        "w_in": nrm(ks[12], (DEPTH, D_MODEL, IN_W), D_MODEL ** -0.5),
        "sink": nrm(ks[13], (DEPTH, N_HEADS), 0.5),
        "w_attn_out": nrm(ks[14], (DEPTH, Q_W, D_MODEL), Q_W ** -0.5),
        "dw_w": nrm(ks[15], (DEPTH, CONV_W, D_CONV), CONV_W ** -0.5),
        "dw_b": nrm(ks[16], (DEPTH, D_CONV), 0.01),
        "conv_ln_g": 1.0 + nrm(ks[17], (DEPTH, D_CONV), 0.02),
        "conv_ln_b": nrm(ks[18], (DEPTH, D_CONV), 0.01),
        "w_conv_out": nrm(ks[19], (DEPTH, D_CONV, D_MODEL), D_CONV ** -0.5),
        "w_out": nrm(ks[20], (DEPTH, D_MODEL, D_MODEL), D_MODEL ** -0.5),
        "w_ffn_up": nrm(ks[21], (DEPTH, D_MODEL, 2 * D_FF), D_MODEL ** -0.5),
        "w_ffn_down": nrm(ks[22], (DEPTH, D_FF, D_MODEL), D_FF ** -0.5),
        "final_g": 1.0 + nrm(ks[23], (D_MODEL,), 0.02),
    }


def reference(x_prompt, x_sample, cache_k, cache_v, cache_conv, c_prompt, c_sample, rel_table,
              w_ada, b_ada, norm1_g, norm2_g, w_in, sink, w_attn_out, dw_w, dw_b, conv_ln_g,
              conv_ln_b, w_conv_out, w_out, w_ffn_up, w_ffn_down, final_g):
    yp = x_prompt
    ys = x_sample
    kp, vp, cp, ks, vs, cs = [], [], [], [], [], []
    for l in range(DEPTH):
        params = (rel_table, w_ada[l], b_ada[l], norm1_g[l], norm2_g[l], w_in[l], sink[l],
                  w_attn_out[l], dw_w[l], dw_b[l], conv_ln_g[l], conv_ln_b[l], w_conv_out[l],
                  w_out[l], w_ffn_up[l], w_ffn_down[l])
        yp, nk, nv, nc = _trunk_layer(yp, c_prompt, None, None, None, *params)
        kp.append(nk); vp.append(nv); cp.append(nc)
        ys, nk, nv, nc = _trunk_layer(ys, c_sample, cache_k[l], cache_v[l], cache_conv[l], *params)
        ks_ = nk
        ks.append(ks_); vs.append(nv); cs.append(nc)
    y_prompt = _rmsnorm(yp, final_g)
    y_sample = _rmsnorm(ys, final_g)
    new_k_prompt = jnp.stack(kp)
    new_v_prompt = jnp.stack(vp)
    new_conv_prompt = jnp.stack(cp)
    new_k_sample = jnp.stack(ks)
    new_v_sample = jnp.stack(vs)
    new_conv_sample = jnp.stack(cs)
    return (y_prompt, y_sample, new_k_prompt, new_v_prompt, new_conv_prompt,
            new_k_sample, new_v_sample, new_conv_sample)
```

```python
import math
from contextlib import ExitStack

import numpy as np
import concourse.bass as bass
import concourse.mybir as mybir
from concourse.bass_utils import run_bass_kernel_spmd

F32 = mybir.dt.float32
BF16 = mybir.dt.bfloat16
AF = mybir.ActivationFunctionType
ALU = mybir.AluOpType
AX = mybir.AxisListType

D = 1024
NT = 10
TC = NT * 128
NQ = TC - 128
DFF = 2816
EPS = 1e-6
NEG = -1e30
SAME_ENG_SYNC = True
import os
SKIP = set(os.environ.get('KSKIP', '').split(','))
STAGE_MARKS = []
NJUNK = int(os.environ.get('NJUNK', '0'))
BIAS_LO = os.environ.get('BIAS_LO', '1') == '1'


class Res:
    def __init__(self, name, excl=False):
        self.name = name
        self.w = {}
        self.r = {}
        self.sem = None
        self.dcnt = 0
        self.excl = excl

    def inherit(self, *olds):
        for o in olds:
            for k, ev in list(o.w.items()) + list(o.r.items()):
                if k not in self.r or self.r[k][1] < ev[1]:
                    self.r[k] = ev
        return self


class Eng:
    def __init__(self, fw, h, name, is_pe=False):
        self.fw = fw
        self.h = h
        self.name = name
        self.sem = fw.new_sem("e_" + name)
        self.cnt = 0
        self.seen = {}
        self.is_pe = is_pe

    def wait(self, ev):
        if ev is None:
            return
        sem, val = ev
        k = id(sem)
        if self.seen.get(k, 0) >= val:
            return
        if sem is self.sem and (self.is_pe or not SAME_ENG_SYNC):
            return
        self.h.wait_ge(sem, val)
        self.seen[k] = val

    def deps(self, reads, writes, wadd):
        for r in reads:
            for ev in list(r.w.values()):
                self.wait(ev)
            if r.excl:
                for ev in list(r.r.values()):
                    if ev[0] is not self.sem:
                        self.wait(ev)
        for w in writes:
            for ev in list(w.w.values()):
                self.wait(ev)
            for ev in list(w.r.values()):
                self.wait(ev)
        for w in wadd:
            for ev in list(w.r.values()):
                self.wait(ev)

    def mark(self, ev, reads, writes, wadd):
        k = id(ev[0])
        for r in reads:
            r.r[k] = ev
        for w in writes:
            w.w = {k: ev}
        for w in wadd:
            w.w[k] = ev

    def op(self, fn, reads=(), writes=(), wadd=()):
        self.deps(reads, writes, wadd)
        inst = fn()
        self.cnt += 1
        inst.then_inc(self.sem, 1)
        ev = (self.sem, self.cnt)
        self.mark(ev, reads, writes, wadd)
        return ev

    def group(self, fns, reads=(), writes=(), wadd=()):
        self.deps(reads, writes, wadd)
        inst = None
        for fn in fns:
            inst = fn()
        self.cnt += 1
        inst.then_inc(self.sem, 1)
        ev = (self.sem, self.cnt)
        self.mark(ev, reads, writes, wadd)
        return ev

    def dma(self, pairs, reads=(), writes=(), wadd=(), **kw):
        self.deps(reads, writes, wadd)
        tgt = writes[0] if writes else (wadd[0] if wadd else reads[0])
        if tgt.sem is None:
            tgt.sem = self.fw.new_sem("d_" + tgt.name)
        for (o, i) in pairs:
            inst = self.h.dma_start(out=o, in_=i, **kw)
            tgt.dcnt += 16
            inst.then_inc(tgt.sem, 16)
        ev = (tgt.sem, tgt.dcnt)
        self.mark(ev, reads, writes, wadd)
        return ev


class FW:
    def __init__(self, nc, stack):
        self.nc = nc
        self.stack = stack
        self.nsem = 0

    def new_sem(self, name):
        self.nsem += 1
        return self.stack.enter_context(self.nc.semaphore(name))

    def sb(self, name, shape, dt):
        return self.stack.enter_context(self.nc.sbuf_tensor(name, shape, dt))

    def ps(self, name, shape, dt):
        return self.stack.enter_context(self.nc.psum_tensor(name, shape, dt))


class Carver:
    def __init__(self, t):
        self.t = t

    def v(self, off, shape, dt):
        esz = 2 if dt == BF16 else 4
        n = int(np.prod(shape))
        assert off % 4 == 0
        a = self.t[:, off // 2: off // 2 + n * esz // 2]
        if dt != BF16:
            a = a.bitcast(dt)
        if len(shape) == 2:
            a = a.rearrange("p (a b) -> p a b", a=shape[0])
        elif len(shape) == 3:
            a = a.rearrange("p (a b c) -> p a b c", a=shape[0], b=shape[1])
        elif len(shape) == 4:
            a = a.rearrange("p (a b c d) -> p a b c d", a=shape[0], b=shape[1], c=shape[2])
        return a


HA = [2 * i for i in range(8)]
HB = [2 * i + 1 for i in range(8)]
SLOT_HEAD = [HA[s // 2] if s % 2 == 0 else HB[s // 2] for s in range(16)]


def build_program(debug=None, n_st=2, upto=99):
    nc = bass.Bass("TRN2", target_bir_lowering=False)

    def din(name, shape):
        return nc.dram_tensor(name, list(shape), F32, kind="ExternalInput").ap()

    def dout(name, shape):
        return nc.dram_tensor(name, list(shape), F32, kind="ExternalOutput").ap()

    xp = din("xp", [2176, D])
    xs = din("xs", [256, D])
    ck = din("ck", [4, 128, 256])
    cv = din("cv", [4, 128, 256])
    cc = din("cc", [4, 30, D])
    cvec = din("cvec", [5, D])
    flags = din("flags", [128, 2])
    vecs = din("vecs", [42, D])
    bgt = din("bgt", [2, D])
    fg = din("fg", [1, D])
    biasg = din("biasg", [128, 16 * 256])
    maskc = din("maskc", [128, 16 * 256])
    sinkrow = din("sinkrow", [1, 16])
    ident_d = din("ident", [128, 128])
    w_ada = din("w_ada", [D, 6 * D])
    w_in = din("w_in", [D, 5632])
    w_ao = din("w_ao", [D, D])
    w_co = din("w_co", [D, D])
    w_o = din("w_o", [D, D])
    w_up = din("w_up", [D, 2 * DFF])
    w_dn = din("w_dn", [DFF, D])

    yp = dout("yp", [2048, D])
    ys = dout("ys", [256, D])
    nkp = dout("nkp", [128, 256])
    nvp = dout("nvp", [128, 256])
    ncp = dout("ncp", [30, D])
    nks = dout("nks", [4, 128, 256])
    nvs = dout("nvs", [4, 128, 256])
    ncs = dout("ncs", [4, 30, D])
    dbg_out = {}
    if debug:
        for name, shape in debug.items():
            dbg_out[name] = dout("dbg_" + name, shape)

    with ExitStack() as st:
        fw = FW(nc, st)
        bhi = fw.sb("bhi", [128, 16, 258], BF16)
        blo = fw.sb("blo", [128, 16, 258], BF16)
        hmfull = fw.sb("hmfull", [128, 128], BF16)
        gtrow = fw.sb("gtrow", [128, 6, D], F32)
        fgrow = fw.sb("fgrow", [128, D], F32)
        identb = fw.sb("identb", [128, 128], BF16)
        identf = fw.sb("identf", [128, 128], F32)
        onesb = fw.sb("onesb", [128, 128], BF16)
        vecT = fw.sb("vecT", [128, 8, 42], F32)
        modT = fw.sb("modT", [128, 4, 8, 5], F32)
        crep = fw.sb("crep", [128, 3, 8, 128], BF16)
        scT = fw.sb("scT", [128, 8, 5], BF16)
        sinkrep = fw.sb("sinkrep", [128, 16], F32)
        nsinkrep = fw.sb("nsinkrep", [128, 16], F32)
        flg = fw.sb("flg", [128, 2], F32)
        epsT = fw.sb("epsT", [128, 1], F32)
        stat = fw.sb("stat", [128, 3, 16], F32)
        wsl = [fw.sb(f"wsl{i}", [128, 5632], BF16) for i in range(2)]
        ystage = [fw.sb(f"ystage{i}", [128, D], F32) for i in range(4)]
        hT = fw.sb("hT", [128, 8, TC], BF16)
        ccb_t = fw.sb("ccb_t", [32, 2, D], BF16)
        POOLB = 90112
        poolt = fw.sb("poolt", [128, POOLB // 2], BF16)
        cv_ = Carver(poolt)
        psum_all = fw.ps("psum_all", [128, 4096], F32)
        banks = [psum_all[:, 512 * i:512 * (i + 1)] for i in range(8)]
        RB = [Res(f"bank{i}", excl=True) for i in range(8)]

        st.enter_context(nc.Block())
        pe = Eng(fw, nc.tensor, "pe", is_pe=True)
        act = Eng(fw, nc.scalar, "act")
        dve = Eng(fw, nc.vector, "dve")
        pool = Eng(fw, nc.gpsimd, "pool")
        sp = Eng(fw, nc.sync, "sp")
        out_evs = []

        R = {}

        def res(name):
            if name not in R:
                R[name] = Res(name)
            return R[name]

        def bankbf(i):
            return banks[i].bitcast(BF16)

        def dbg(name, ap, rs):
            if debug and name in debug:
                out_evs.append(pool.dma([(dbg_out[name], ap)], reads=rs))

        wres = [Res("wsl0"), Res("wsl1")]
        wstate = {"n_issued": 0, "n_used": 0, "plan": []}

        def wplan_add(fn):
            wstate["plan"].append(fn)

        def wprefetch():
            i = wstate["n_issued"]
            if i >= len(wstate["plan"]):
                return
            if i - wstate["n_used"] >= 2:
                return
            s = i % 2
            pairs = wstate["plan"][i](wsl[s])
            pool.dma(pairs, writes=[wres[s]])
            wstate["n_issued"] += 1

        def wget():
            i = wstate["n_used"]
            while wstate["n_issued"] <= i:
                wprefetch()
            s = i % 2
            wstate["n_used"] += 1
            return wsl[s], wres[s]

        def wdone():
            wprefetch()
            wprefetch()

        def wview(slot, kc_n, cols):
            return slot[:, 0:kc_n * cols].rearrange("p (k c) -> p k c", k=kc_n)

        def plain_group(wd, c0, ncols=512):
            def f(slot):
                return [(wview(slot, 8, ncols), wd[:, c0:c0 + ncols].rearrange("(k p) c -> p k c", p=128))]
            return f

        def pair_group(wd, c0, c1):
            def f(slot):
                v = wview(slot, 8, 512)
                return [(v[:, :, 0:256], wd[:, c0:c0 + 256].rearrange("(k p) c -> p k c", p=128)),
                        (v[:, :, 256:512], wd[:, c1:c1 + 256].rearrange("(k p) c -> p k c", p=128))]
            return f

        def q_group(h):
            def f(slot):
                v = slot[:, 0:4096].rearrange("p (k il hf d) -> p k il hf d", k=8, il=4, hf=2)
                prs = []
                for hf in range(2):
                    for il in range(4):
                        c0 = h * 512 + hf * 256 + il * 64
                        src = w_in[:, c0:c0 + 64].rearrange("(k p) d -> p k d", p=128)
                        prs.append((v[:, :, il, hf, :], src))
                return prs
            return f

        def ao_group(g):
            def f(slot):
                v = slot[:, 0:4096].rearrange("p (kh kl c) -> p kh kl c", kh=2, kl=4)
                prs = []
                for hf in range(2):
                    for kh in range(2):
                        r0 = kh * 512 + hf * 256
                        src = w_ao[r0:r0 + 256, g * 512:(g + 1) * 512].rearrange("(kl p) c -> p kl c", p=64)
                        prs.append((v[hf * 64:(hf + 1) * 64, kh, :, :], src))
                return prs
            return f

        def dn_group(ch, kh):
            def f(slot):
                return [(wview(slot, 11, 512), w_dn[kh * 1408:(kh + 1) * 1408, ch * 512:(ch + 1) * 512].rearrange("(k p) c -> p k c", p=128))]
            return f

        bankset_ctr = [0]

        def next_bankset():
            bankset_ctr[0] += 1
            return (0, 1, 2) if bankset_ctr[0] % 2 else (3, 4, 5)

        def mm_a(fcs, nblocks, lhsT_fn, rhs_fn, reads, evac_fn, kn=8, first_kc_reads=None, first_nb_reads=None):
            for idx, fc in enumerate(fcs):
                bs = next_bankset()
                rbs = [RB[bs[bi]] for bi in range(len(nblocks))]

                def mm(kc, bi, n0, n1, fc=fc, bs=bs):
                    return lambda: nc.tensor.matmul(banks[bs[bi]][:, 0:n1 - n0], lhsT=lhsT_fn(kc, fc), rhs=rhs_fn(kc, n0, n1),
                                                    start=(kc == 0), stop=(kc == kn - 1))
                if idx == 0 and first_kc_reads is not None:
                    for kc in range(kn):
                        fns = [mm(kc, bi, n0, n1) for bi, (n0, n1) in enumerate(nblocks)]
                        if kc == 0:
                            pe.group(fns, reads=first_kc_reads(kc), writes=rbs)
                        else:
                            pe.group(fns, reads=first_kc_reads(kc), wadd=rbs)
                elif idx == 0 and first_nb_reads is not None:
                    for bi, (n0, n1) in enumerate(nblocks):
                        fns = [mm(kc, bi, n0, n1) for kc in range(kn)]
                        pe.group(fns, reads=first_nb_reads(bi), writes=[rbs[bi]])
                else:
                    fns = []
                    for kc in range(kn):
                        for bi, (n0, n1) in enumerate(nblocks):
                            fns.append(mm(kc, bi, n0, n1))
                    pe.group(fns, reads=reads, writes=rbs)
                for bi, (n0, n1) in enumerate(nblocks):
                    evac_fn(fc, bi, n0, n1, banks[bs[bi]][:, 0:n1 - n0], RB[bs[bi]])

        rot = {"b": 0, "e": 0}

        def next_bank():
            rot["b"] = (rot["b"] + 1) % 8
            return rot["b"]

        def alt_eng():
            rot["e"] += 1
            return act if rot["e"] % 2 else dve

        def copy_on(eng, out, in_, reads, writes=(), wadd=(), scale=None):
            if eng is act:
                if scale is None:
                    return act.op(lambda: nc.scalar.activation(out=out, in_=in_, func=AF.Copy), reads=reads, writes=writes, wadd=wadd)
                return act.op(lambda: nc.scalar.activation(out=out, in_=in_, func=AF.Copy, scale=scale), reads=reads, writes=writes, wadd=wadd)
            if scale is None:
                return eng.op(lambda: eng.h.tensor_copy(out=out, in_=in_), reads=reads, writes=writes, wadd=wadd)
            return eng.op(lambda: eng.h.tensor_scalar(out=out, in0=in_, scalar1=scale, scalar2=None, op0=ALU.mult),
                          reads=reads, writes=writes, wadd=wadd)

        PA_ATT, PA_R1, PA_G = 0, 18432, 71680
        pro_vt = cv_.v(PA_R1, [D], F32)
        pro_ct = cv_.v(PA_R1 + 4096, [D], F32)
        pro_cs = cv_.v(PA_R1 + 8192, [D], F32)
        pro_mk = cv_.v(PA_R1 + 20480, [16, 256], F32)
        Rc = res("consts")
        sp.dma([(identf[:, :], ident_d)], writes=[res("identf")])
        pool.dma([(identb[:, :], ident_d)], writes=[res("identb")])
        sp.dma([(pro_vt[0:42, :], vecs)], writes=[res("pro_vt")])
        sp.dma([(pro_ct[0:5, :], cvec)], writes=[res("pro_ct")])
        sp.dma([(flg[:, :], flags)], writes=[res("flg")])
        sp.dma([(sinkrep[:, :], sinkrow.to_broadcast([128, 16]))], writes=[res("sinkrep")])
        sp.dma([(fgrow[:, :], fg.to_broadcast([128, D]))], writes=[res("fgrow")])
        sp.dma([(ystage[0][:, :], bgt[0:1, :].to_broadcast([128, D])), (ystage[1][:, :], bgt[1:2, :].to_broadcast([128, D]))],
               writes=[res("bgtrep")])
        pro_bg = cv_.v(PA_R1 + 36864, [16, 256], F32)
        dve.op(lambda: nc.vector.memset(onesb[:, :], 1.0), writes=[res("onesb")])
        dve.op(lambda: nc.vector.memset(epsT[:, :], EPS), writes=[res("epsT")])
        def build_bias_tables():
            sp.dma([(pro_bg, biasg.rearrange("p (s j) -> p s j", s=16))], writes=[res("pro_bg")])
            sp.dma([(pro_mk, maskc.rearrange("p (s j) -> p s j", s=16))], writes=[res("pro_mk")])
            pool.op(lambda: nc.gpsimd.tensor_tensor(out=pro_bg, in0=pro_bg, in1=pro_mk, op=ALU.add),
                    reads=[res("pro_mk"), res("pro_bg")], writes=[res("pro_bg")])
            dve.op(lambda: nc.vector.memset(bhi[:, :, 256:258], 0.0), wadd=[res("biasm")])
            dve.op(lambda: nc.vector.memset(blo[:, :, 256:258], 0.0), wadd=[res("biasm")])
            act.op(lambda: nc.scalar.activation(out=bhi[:, :, 0:256], in_=pro_bg, func=AF.Copy), reads=[res("pro_bg"), res("biasm")], wadd=[res("biasm")])
            pool.op(lambda: nc.gpsimd.tensor_tensor(out=pro_mk, in0=pro_bg, in1=bhi[:, :, 0:256], op=ALU.subtract),
                    reads=[res("pro_bg"), res("biasm")], writes=[res("pro_mk")])
            act.op(lambda: nc.scalar.activation(out=blo[:, :, 0:256], in_=pro_mk, func=AF.Copy), reads=[res("pro_mk")], wadd=[res("biasm")])
            dve.op(lambda: nc.vector.tensor_copy(out=bhi[:, :, 256:257], in_=sinkrep[:, :].unsqueeze(2)), reads=[res("sinkrep"), res("biasm")], wadd=[res("biasm")])
            dve.op(lambda: nc.vector.tensor_tensor(out=nsinkrep[:, :].unsqueeze(2), in0=sinkrep[:, :].unsqueeze(2), in1=bhi[:, :, 256:257], op=ALU.subtract),
                   reads=[res("sinkrep"), res("biasm")], writes=[res("nsinkrep")])
            dve.op(lambda: nc.vector.tensor_copy(out=blo[:, :, 256:257], in_=nsinkrep[:, :].unsqueeze(2)), reads=[res("nsinkrep"), res("biasm")], wadd=[res("biasm")])

        dve.op(lambda: nc.vector.tensor_copy(out=hmfull[:, :], in_=flg[:, 0:1].to_broadcast([128, 128])), reads=[res("flg")], writes=[res("hmrow")])
        pe.group([lambda c=c: nc.tensor.transpose(out=banks[6][:, c * 42:(c + 1) * 42], in_=pro_vt[0:42, c * 128:(c + 1) * 128],
                                                   identity=identf[0:42, 0:42]) for c in range(8)],
                 reads=[res("pro_vt"), res("identf")], writes=[RB[6]])
        dve.op(lambda: nc.vector.tensor_copy(out=vecT[:, :, :], in_=banks[6][:, 0:336].rearrange("p (c v) -> p c v", c=8)),
               reads=[RB[6]], writes=[res("vecT")])
        act.op(lambda: nc.scalar.activation(out=pro_cs[0:5, :], in_=pro_ct[0:5, :], func=AF.Silu),
               reads=[res("pro_ct")], writes=[res("pro_cs")])
        pe.group([lambda c=c: nc.tensor.transpose(out=banks[7][:, c * 5:(c + 1) * 5], in_=pro_cs[0:5, c * 128:(c + 1) * 128],
                                                   identity=identf[0:5, 0:5]) for c in range(8)],
                 reads=[res("pro_cs"), res("identf")], writes=[RB[7]])
        dve.op(lambda: nc.vector.tensor_copy(out=scT[:, :, :], in_=banks[7][:, 0:40].rearrange("p (c v) -> p c v", c=8)),
               reads=[RB[7]], writes=[res("scT")])
        for ty, (ba, bb) in enumerate([(0, 0), (1, 2), (3, 4)]):
            dve.op(lambda ty=ty, ba=ba: nc.vector.tensor_copy(out=crep[:, ty, :, 0:64], in_=scT[:, :, ba:ba + 1].to_broadcast([128, 8, 64])),
                   reads=[res("scT")], wadd=[res("crep")])
            dve.op(lambda ty=ty, bb=bb: nc.vector.tensor_copy(out=crep[:, ty, :, 64:128], in_=scT[:, :, bb:bb + 1].to_broadcast([128, 8, 64])),
                   reads=[res("scT")], wadd=[res("crep")])

        def mod_scalars(q, which, gis=(0, 1), bank_fn=None):
            gvec = 0 if q < 3 else 1
            for gi in gis:
                slot, wr = wget()
                v = wview(slot, 8, 512)
                for fc in range(4):
                    c = gi * 4 + fc
                    b = (bank_fn or next_bank)()
                    pe.group([lambda kc=kc, fc=fc, b=b: nc.tensor.matmul(
                        banks[b][:, 0:5], lhsT=v[:, kc, fc * 128:(fc + 1) * 128], rhs=scT[:, kc, :],
                        start=(kc == 0), stop=(kc == 7)) for kc in range(8)],
                        reads=[wr, res("scT")], writes=[RB[b]])
                    dst = modT[:, which, c, :]
                    if q in (0, 3):
                        dve.op(lambda b=b, dst=dst, c=c: nc.vector.tensor_scalar(
                            out=dst, in0=banks[b][:, 0:5], scalar1=vecT[:, c, 5 + q:6 + q], scalar2=None, op0=ALU.add),
                            reads=[RB[b], res("vecT")], wadd=[res("modT")])
                    else:
                        dve.op(lambda b=b, dst=dst, c=c: nc.vector.tensor_scalar(
                            out=dst, in0=banks[b][:, 0:5], scalar1=vecT[:, c, 5 + q:6 + q], scalar2=1.0, op0=ALU.add, op1=ALU.add),
                            reads=[RB[b], res("vecT")], wadd=[res("modT")])
                        dve.op(lambda dst=dst, c=c: nc.vector.tensor_scalar(
                            out=dst, in0=dst, scalar1=vecT[:, c, gvec:gvec + 1], scalar2=None, op0=ALU.mult),
                            reads=[res("vecT"), res("modT")], wadd=[res("modT")])
                wdone()

        def mod_rows(which, gis=(0, 1), bank_fn=None):
            for gi in gis:
                slot, wr = wget()
                v = wview(slot, 8, 512)
                for ty in range(3):
                    b = (bank_fn or next_bank)()
                    pe.group([lambda kc=kc, ty=ty, b=b: nc.tensor.matmul(
                        banks[b][:, :], lhsT=crep[:, ty, kc, :], rhs=v[:, kc, :],
                        start=(kc == 0), stop=(kc == 7)) for kc in range(8)],
                        reads=[wr, res("crep")], writes=[RB[b]])
                    dve.op(lambda b=b, ty=ty, gi=gi: nc.vector.tensor_tensor(
                        out=gtrow[:, ty * 2 + which, gi * 512:(gi + 1) * 512], in0=banks[b][:, :],
                        in1=ystage[which][:, gi * 512:(gi + 1) * 512], op=ALU.add),
                        reads=[RB[b], res("bgtrep")], wadd=[res("gtrow")])
                wdone()

        for s_ in range(n_st):
            if s_ == 0:
                for q in (0, 1):
                    wplan_add(plain_group(w_ada, q * 1024))
                    wplan_add(plain_group(w_ada, q * 1024 + 512))
            wplan_add(plain_group(w_in, 0)); wplan_add(plain_group(w_in, 512)); wplan_add(plain_group(w_in, 1024))
            wplan_add(plain_group(w_in, 3584)); wplan_add(plain_group(w_in, 4096))
            wplan_add(plain_group(w_ao, 0)); wplan_add(plain_group(w_ao, 512))
            for i in range(4):
                wplan_add(pair_group(w_in, 1536 + 256 * i, 2560 + 256 * i))
            if s_ == 0:
                for c0 in (2048, 2560, 3072, 3584, 4096, 4608, 5120, 5632):
                    wplan_add(plain_group(w_ada, c0))
            wplan_add(plain_group(w_in, 4608)); wplan_add(plain_group(w_in, 5120))
            wplan_add(plain_group(w_co, 0)); wplan_add(plain_group(w_co, 512))
            wplan_add(plain_group(w_o, 0)); wplan_add(plain_group(w_o, 512))
            for i in range(11):
                wplan_add(pair_group(w_up, 256 * i, DFF + 256 * i))
            for ch in range(2):
                for kh in range(2):
                    wplan_add(dn_group(ch, kh))

        wprefetch()
        wprefetch()
        mod_scalars(0, 1)
        mod_scalars(1, 0)

        nctr = [0]

        def norm_to_featmajor(x_ap, x_res, dstT, col0, Awhich, Bwhich, batches, dst_res, xn_bufs, sq_buf, split=False, nxn=2, s1mode=False):
            k = nctr[0] % 16
            nctr[0] += 1
            ss = stat[:, 0, k:k + 1]
            sd = stat[:, 1, k:k + 1]
            rs_ = stat[:, 2, k:k + 1]
            xn = xn_bufs[k % nxn]
            xnr = res(f"xn{k % nxn}")
            sqr = res("sqscr")
            rstat = res(f"stat{k}")
            act.op(lambda: nc.scalar.activation(out=sq_buf, in_=x_ap, func=AF.Square, accum_out=ss),
                   reads=[x_res], writes=[sqr, rstat])
            act.op(lambda: nc.scalar.activation(out=sd, in_=ss, func=AF.Sqrt, scale=1.0 / D, bias=epsT[:, 0:1]),
                   reads=[res("epsT")], writes=[rstat])
            dve.op(lambda: nc.vector.reciprocal(out=rs_, in_=sd), writes=[rstat])
            if s1mode:
                act.op(lambda: nc.scalar.activation(out=xn, in_=x_ap, func=AF.Copy, scale=rs_),
                       reads=[x_res, rstat], writes=[xnr])
            else:
                dve.op(lambda: nc.vector.tensor_scalar(out=xn, in0=x_ap, scalar1=rs_, scalar2=None, op0=ALU.mult),
                       reads=[x_res, rstat], writes=[xnr])
            b = (nctr[0] % 2) + 6
            pb = bankbf(b)

            def a2():
                pe.group([lambda c=c: nc.tensor.transpose(out=pb[:, c * 128:(c + 1) * 128], in_=xn[:, c * 128:(c + 1) * 128],
                                                           identity=identb[:, :]) for c in range(8)],
                         reads=[xnr, res("identb")], writes=[RB[b]])
            if not split:
                a2()
            eng = dve if s1mode else alt_eng()

            def fin():
              for c in range(8):
                for (o0, o1, bi) in batches:
                    src = pb[:, c * 128 + o0:c * 128 + o1]
                    dst = dstT[:, c, col0 + o0:col0 + o1]
                    A = modT[:, Awhich, c, bi:bi + 1]
                    B = modT[:, Bwhich, c, bi:bi + 1]
                    if eng is act:
                        act.op(lambda src=src, dst=dst, A=A, B=B: nc.scalar.activation(out=dst, in_=src, func=AF.Identity, bias=B, scale=A),
                               reads=[RB[b], res("modT")], wadd=[dst_res])
                    else:
                        dve.op(lambda src=src, dst=dst, A=A, B=B: nc.vector.tensor_scalar(out=dst, in0=src, scalar1=A, scalar2=B, op0=ALU.mult, op1=ALU.add),
                               reads=[RB[b], res("modT")], wadd=[dst_res])
            return (a2, fin) if split else fin

        carved = [res("pro_vt"), res("pro_ct"), res("pro_cs"), res("pro_mk"), res("pro_bg")]

        def mk(name):
            r_ = Res(name).inherit(*carved)
            carved.append(r_)
            return r_

        for s_ in range(n_st):
            tag = f"s{s_}_"
            sb_idx = (1 + 2 * s_, 2 + 2 * s_)
            sbl = (2 * s_, 2 * s_ + 1)
            ty_s = 1 + s_
            hres = [Res(tag + f"hT{t}") for t in range(NT)]
            if s_ > 0:
                for t in range(NT):
                    hres[t].inherit(*prev_h2res)
            STAGE_MARKS.append((f"{s_}:S1", pe.cnt, act.cnt, dve.cnt, pool.cnt))
            xsl = [cv_.v(PA_R1 + 4096 * i, [D], F32) for i in range(3)]
            xslr = [mk(tag + f"xsl{i}") for i in range(3)]
            xnb = [cv_.v(PA_R1 + 12288 + 2048 * i, [D], BF16) for i in range(2)]
            sqb = cv_.v(PA_R1 + 16384, [D], BF16)
            R["xn0"] = mk(tag + "xn0"); R["xn1"] = mk(tag + "xn1"); R["sqscr"] = mk(tag + "sqscr")
            for t in range(NT):
                sl = t % 3
                if t < 9:
                    src = xp[1024 * s_ + 128 * t: 1024 * s_ + 128 * t + 128, :]
                    batches = [(0, 128, 0)]
                else:
                    src = xs[128 * s_:128 * s_ + 128, :]
                    batches = [(0, 64, sb_idx[0]), (64, 128, sb_idx[1])]
                sp.dma([(xsl[sl], src)], writes=[xslr[sl]])
                fin_ = norm_to_featmajor(xsl[sl], xslr[sl], hT, 128 * t, 0, 1, batches, hres[t], xnb, sqb, s1mode=True)
                if t > 0:
                    prev_fin()
                prev_fin = fin_
            prev_fin()
            if s_ == 0:
                build_bias_tables()
            if debug and s_ == 0:
                dbg("hT", hT[:, :, :], hres)
            if upto <= 1:
                break

            STAGE_MARKS.append((f"{s_}:S2a", pe.cnt, act.cnt, dve.cnt, pool.cnt))
            qT = cv_.v(PA_R1, [8, NQ], BF16)
            kT = cv_.v(PA_R1 + 18432, [2, TC], BF16)
            kTB = cv_.v(PA_R1 + 23552, [2, TC], BF16)
            kTs = cv_.v(PA_R1 + 28672, [2, 2, 192], BF16)
            kTsB = cv_.v(PA_R1 + 30208, [2, 2, 192], BF16)
            Vt = cv_.v(PA_R1 + 31744, [NT, 256], BF16)
            Vc = cv_.v(PA_R1 + 36864, [2, 256], BF16)
            Vn = cv_.v(PA_R1 + 37888, [2, 256], BF16)
            kcb = cv_.v(PA_R1 + 38912, [2, 256], BF16)
            tokst = [cv_.v(PA_R1 + 39936 + 1024 * i, [256], F32) for i in range(1)]
            SW = 257
            spb = [cv_.v(PA_G + 2056 * i, [2, SW], F32) for i in range(3)]
            pbf = [cv_.v(PA_G + 6168 + 2056 * i, [2, SW], F32) for i in range(3)]
            pnb = [cv_.v(PA_G + 12336 + 1024 * i, [2, 256], BF16) for i in range(3)]
            PTs = [cv_.v(PA_G + 15408 + 1024 * i, [2, 2, 128], BF16) for i in range(2)]
            attT = cv_.v(PA_ATT, [8, NQ], BF16)
            qres = [mk(tag + f"qT{t}") for t in range(9)]
            kres = [mk(tag + f"kT{t}") for t in range(NT)]
            vres = [mk(tag + f"V{t}") for t in range(NT)]
            ksres = mk(tag + "kTs")
            vsres = mk(tag + "Vs")
            kcres = mk(tag + "kcb")
            tokres = mk(tag + "tokst")
            kbres = mk(tag + "kTB")
            ksbres = mk(tag + "kTsB")
            spr = [mk(tag + f"sp{i}") for i in range(3)]
            pbr = [mk(tag + f"pb{i}") for i in range(3)]
            pnr = [mk(tag + f"pn{i}") for i in range(3)]
            ptr_ = [mk(tag + f"PTs{i}") for i in range(2)]
            ares = [mk(tag + f"attT{t}") for t in range(9)]
            smA = [mk(tag + f"smA{i}") for i in range(3)]
            smB = [mk(tag + f"smB{i}") for i in range(3)]
            smC = [mk(tag + f"smC{i}") for i in range(3)]
            smst = cv_.v(PA_R1 + 40960, [3, 12], F32)

            nb_q = [(128, 640), (640, 1152), (1152, 1280)]
            nb_all = [(0, 512), (512, 1024), (1024, 1280)]

            def tiles_of(n0, n1):
                return list(range(n0 // 128, (n1 + 127) // 128))

            ccres = res("ccb_t")
            pool.dma([(ccb_t[0:30, bb, :], cc[sbl[bb]]) for bb in range(2)], writes=[ccres])
            pool.dma([(kcb[:, bb, :], ck[sbl[bb]]) for bb in range(2)], writes=[kcres])
            pool.dma([(Vc[:, bb, :], cv[sbl[bb]]) for bb in range(2)], wadd=[vsres])
            for h in range(2):
                slot, wr = wget()
                v = wview(slot, 8, 512)

                def evac_q(fc, bi, n0, n1, psrc, pres, h=h):
                    c = 4 * h + fc
                    eng = alt_eng()
                    copy_on(eng, qT[:, c, n0 - 128:n1 - 128], psrc, reads=[pres], wadd=[qres[t - 1] for t in tiles_of(n0, n1)], scale=0.125)
                mm_a(range(4), nb_q, lambda kc, fc: v[:, kc, fc * 128:(fc + 1) * 128],
                     lambda kc, n0, n1: hT[:, kc, n0:n1], [wr] + hres, evac_q,
                     first_nb_reads=((lambda bi, wr=wr: [wr] + [hres[t] for t in tiles_of(*nb_q[bi])]) if h == 0 else None))
                wdone()
            if 'kv' in SKIP:
                break
            slot, wr = wget()
            v = wview(slot, 8, 512)

            def evac_k(fc, bi, n0, n1, psrc, pres):
                eng = alt_eng()
                copy_on(eng, kT[:, fc, n0:n1], psrc, reads=[pres], wadd=[kres[t] for t in tiles_of(n0, n1)])
            if 'kmm' not in SKIP:
                mm_a(range(2), nb_all, lambda kc, fc: v[:, kc, fc * 128:(fc + 1) * 128], lambda kc, n0, n1: hT[:, kc, n0:n1],
                     [wr] + hres, evac_k)
            if 'kts' not in SKIP:
                for bb in range(2):
                    copy_on(alt_eng(), kTs[:, :, bb, 128:192], kT[:, :, 1152 + 64 * bb:1216 + 64 * bb], reads=[kres[9]], wadd=[ksres])
            sp.dma([(kTB[0:64], kT[64:128]), (kTB[64:128], kT[0:64])], reads=kres, writes=[kbres])
            for t in (range(NT) if 'vmm' not in SKIP else []):
                if t < 9:
                    b = next_bank()
                    pe.group([lambda kc=kc, t=t, b=b: nc.tensor.matmul(banks[b][:, 0:256], lhsT=hT[:, kc, 128 * t:128 * t + 128],
                                                                         rhs=v[:, kc, 256:512], start=(kc == 0), stop=(kc == 7)) for kc in range(8)],
                             reads=[wr, hres[t]], writes=[RB[b]])
                    copy_on(alt_eng(), Vt[:, t, :], banks[b][:, 0:256], reads=[RB[b]], writes=[vres[t]])
                    if t == 8 and s_ == n_st - 1 and 'tok8' not in SKIP:
                        dve.op(lambda b=b: nc.vector.tensor_copy(out=tokst[0], in_=banks[b][:, 0:256]), reads=[RB[b]], writes=[tokres])
                        out_evs.append(sp.dma([(nvp, tokst[0])], reads=[tokres]))
                        b2 = next_bank()
                        pe.group([lambda kc=kc, t=t, b2=b2: nc.tensor.matmul(banks[b2][:, 0:256], lhsT=hT[:, kc, 128 * t:128 * t + 128],
                                                                               rhs=v[:, kc, 0:256], start=(kc == 0), stop=(kc == 7)) for kc in range(8)],
                                 reads=[wr, hres[t]], writes=[RB[b2]])
                        dve.op(lambda b2=b2: nc.vector.tensor_copy(out=tokst[0], in_=banks[b2][:, 0:256]), reads=[RB[b2]], writes=[tokres])
                        out_evs.append(sp.dma([(nkp, tokst[0])], reads=[tokres]))
                elif 'tok9' not in SKIP:
                    for bb in range(2):
                        for which in range(2):
                            b = next_bank()
                            c0 = 256 if which == 0 else 0
                            pe.group([lambda kc=kc, b=b, bb=bb, c0=c0: nc.tensor.matmul(
                                banks[b][0:64, 0:256], lhsT=hT[:, kc, 128 * 9 + 64 * bb:128 * 9 + 64 * bb + 64],
                                rhs=v[:, kc, c0:c0 + 256], start=(kc == 0), stop=(kc == 7)) for kc in range(8)],
                                reads=[wr, hres[9]], writes=[RB[b]])
                            if which == 0:
                                copy_on(act, Vn[0:64, bb, :], banks[b][0:64, 0:256], reads=[RB[b]], wadd=[vsres])
                            dve.op(lambda b=b: nc.vector.tensor_copy(out=tokst[0][0:64, :], in_=banks[b][0:64, 0:256]), reads=[RB[b]], writes=[tokres])
                            dst = (nvs if which == 0 else nks)[sbl[bb], 64:128, :]
                            out_evs.append(sp.dma([(dst, tokst[0][0:64, :])], reads=[tokres]))
            wdone()
            if s_ == 0 and 'cc' not in SKIP:
                rcp = res("cache_copy")
                out_evs.append(sp.dma([(nks[:, 0:64, :], ck[:, 64:128, :]), (nvs[:, 0:64, :], cv[:, 64:128, :])], reads=[rcp]))
            for bb in (range(2) if 'kcb' not in SKIP else []):
                b = next_bank()
                pb = bankbf(b)
                pe.group([lambda c=c, bb=bb, pb=pb: nc.tensor.transpose(out=pb[:, c * 128:(c + 1) * 128], in_=kcb[:, bb, c * 128:(c + 1) * 128],
                                                                         identity=identb[:, :]) for c in range(2)],
                         reads=[kcres, res("identb")], writes=[RB[b]])
                copy_on(alt_eng(), kTs[:, :, bb, 0:128], pb[:, 0:256].rearrange("p (c j) -> p c j", c=2), reads=[RB[b]], wadd=[ksres])
            if 'swap' not in SKIP:
                sp.dma([(kTsB[0:64], kTs[64:128]), (kTsB[64:128], kTs[0:64])], reads=[ksres], writes=[ksbres])
            if debug and s_ == 0:
                dbg("qT", qT, qres)
                dbg("kT", kT, kres)
                dbg("Vt", Vt, vres)
            if upto <= 2:
                break

            STAGE_MARKS.append((f"{s_}:S2b", pe.cnt, act.cnt, dve.cnt, pool.cnt))
            qtiles = []
            for t in range(1, 9):
                qtiles.append(dict(nq=128, kw=256, qc0=128 * (t - 1), t=t, kind="p", bb=0, oc0=0, oi=t - 1))
            for bb in range(2):
                qtiles.append(dict(nq=64, kw=192, qc0=1024 + 64 * bb, t=9, kind="s", bb=bb, oc0=64 * bb, oi=8))
            steps = [(qi, i) for qi in range(len(qtiles)) for i in range(8)]
            NS = len(steps)
            OB = (6, 7)

            def emit_qk(n):
                qi, i = steps[n]
                q = qtiles[qi]
                sb2 = 2 * (n % 2)
                g = i // 2
                fns = []
                rd = [qres[q["t"] - 1]]
                for hf in range(2):
                    base = 64 * hf
                    lhsT = qT[base:base + 64, i, q["qc0"]:q["qc0"] + q["nq"]]
                    use_b = (g % 2) != hf
                    if q["kind"] == "p":
                        t = q["t"]
                        src = kTB if use_b else kT
                        rhs = src[base:base + 64, g // 2, 128 * t - 128:128 * t + 128]
                        rd += [kbres] if use_b else [kres[t - 1], kres[t]]
                    else:
                        src = kTsB if use_b else kTs
                        rhs = src[base:base + 64, g // 2, q["bb"], :]
                        rd += [ksbres] if use_b else [ksres]
                    nq_, kw_ = q["nq"], q["kw"]
                    h_ = 2 * i + hf
                    bk = banks[sb2 + hf]
                    if kw_ == 256:
                        fns.append(lambda bk=bk, h_=h_: nc.tensor.matmul(bk[:, 0:258], lhsT=identb[:, :], rhs=bhi[:, h_, 0:258], start=True, stop=False))
                        if s_ == 0 and q["t"] == 1:
                            fns.append(lambda bk=bk: nc.tensor.matmul(bk[:, 0:128], lhsT=identb[:, :], rhs=hmfull[:, :], start=False, stop=False))
                        if BIAS_LO:
                            fns.append(lambda bk=bk, h_=h_: nc.tensor.matmul(bk[:, 0:258], lhsT=identb[:, :], rhs=blo[:, h_, 0:258], start=False, stop=False))
                    else:
                        pb_ = 64 * hf
                        c0_ = 64 * hf
                        idn = identb[pb_:pb_ + 64, pb_:pb_ + 64]
                        fns.append(lambda bk=bk, idn=idn, h_=h_, pb_=pb_, c0_=c0_: nc.tensor.matmul(
                            bk[0:64, 0:192], lhsT=idn, rhs=bhi[pb_:pb_ + 64, h_, c0_:c0_ + 192], start=True, stop=False))
                        if BIAS_LO:
                            fns.append(lambda bk=bk, idn=idn, h_=h_, pb_=pb_, c0_=c0_: nc.tensor.matmul(
                                bk[0:64, 0:192], lhsT=idn, rhs=blo[pb_:pb_ + 64, h_, c0_:c0_ + 192], start=False, stop=False))
                        fns.append(lambda bk=bk, idn=idn, h_=h_, pb_=pb_: nc.tensor.matmul(
                            bk[0:64, 192:194], lhsT=idn, rhs=bhi[pb_:pb_ + 64, h_, 256:258], start=False, stop=False))
                        if BIAS_LO:
                            fns.append(lambda bk=bk, idn=idn, h_=h_, pb_=pb_: nc.tensor.matmul(
                                bk[0:64, 192:194], lhsT=idn, rhs=blo[pb_:pb_ + 64, h_, 256:258], start=False, stop=False))
                    fns.append(lambda bk=bk, lhsT=lhsT, rhs=rhs, nq_=nq_, kw_=kw_: nc.tensor.matmul(
                        bk[0:nq_, 0:kw_], lhsT=lhsT, rhs=rhs, start=False, stop=True))
                    for _j in range(NJUNK):
                        fns.append(lambda bk=bk: nc.tensor.matmul(bk[:, 260:510], lhsT=identb[:, :], rhs=hT[:, 0, 0:250], start=False, stop=False))
                rd += [res("biasm"), res("identb"), res("hmrow"), res("onesb")]
                pe.group(fns, reads=rd + hres[0:2], writes=[RB[sb2], RB[sb2 + 1]])

            def emit_sm1(n):
                qi, i = steps[n]
                q = qtiles[qi]
                nq, kw = q["nq"], q["kw"]
                sl = n % 3
                sb2 = 2 * (n % 2)
                mx = smst[0:nq, sl, 0:2]
                negm = smst[0:nq, sl, 2:4]
                for hf in range(2):
                    dve.op(lambda hf=hf: nc.vector.tensor_reduce(out=smst[0:nq, sl, hf:hf + 1], in_=banks[sb2 + hf][0:nq, 0:kw + 1], axis=AX.X, op=ALU.max),
                           reads=[RB[sb2 + hf]], wadd=[smA[sl]])
                dve.op(lambda: nc.vector.tensor_scalar(out=negm, in0=mx, scalar1=-1.0, scalar2=None, op0=ALU.mult),
                       reads=[smA[sl]], writes=[smA[sl]])

            def emit_sm2(n):
                qi, i = steps[n]
                q = qtiles[qi]
                nq, kw = q["nq"], q["kw"]
                sl = n % 3
                sb2 = 2 * (n % 2)
                for hf in range(2):
                    act.op(lambda hf=hf: nc.scalar.activation(out=pbf[sl][0:nq, hf, 0:kw + 1], in_=banks[sb2 + hf][0:nq, 0:kw + 1], func=AF.Exp,
                                                              bias=smst[0:nq, sl, 2 + hf:3 + hf], accum_out=smst[0:nq, sl, 4 + hf:5 + hf]),
                           reads=[RB[sb2 + hf], smA[sl]], wadd=[pbr[sl], smB[sl]])

            def emit_sm3(n):
                qi, i = steps[n]
                q = qtiles[qi]
                nq, kw = q["nq"], q["kw"]
                sl = n % 3
                rinv = smst[0:nq, sl, 10:12]
                dve.op(lambda: nc.vector.reciprocal(out=rinv, in_=smst[0:nq, sl, 4:6]), reads=[smB[sl]], writes=[smC[sl]])
                pool.op(lambda: nc.gpsimd.tensor_tensor(out=pnb[sl][0:nq, :, 0:kw], in0=pbf[sl][0:nq, :, 0:kw],
                                                        in1=rinv.unsqueeze(2).to_broadcast([nq, 2, kw]), op=ALU.mult),
                        reads=[pbr[sl], smC[sl]], writes=[pnr[sl]])

            def emit_tr(n):
                qi, i = steps[n]
                q = qtiles[qi]
                nq, kw = q["nq"], q["kw"]
                sl = n % 3
                ps_ = n % 2
                pts = n % 2
                ptb = 4 + ps_
                ptv = bankbf(ptb)[:, 0:512].rearrange("p (h j q) -> p h j q", h=2, j=2)
                fns = []
                for hf in range(2):
                    for jh in range(2):
                        jw = min(128, kw - 128 * jh)
                        fns.append(lambda hf=hf, jh=jh, jw=jw: nc.tensor.transpose(
                            out=ptv[0:jw, hf, jh, 0:nq], in_=pnb[sl][0:nq, hf, 128 * jh:128 * jh + jw], identity=identb[0:nq, 0:nq]))
                pe.group(fns, reads=[pnr[sl], res("identb")], writes=[RB[ptb]])
                ev_eng = act if (n % 2 == 0) else dve
                if kw == 256:
                    copy_on(ev_eng, PTs[pts].rearrange("p h j q -> p (h j q)"), bankbf(ptb)[:, 0:512], reads=[RB[ptb]], writes=[ptr_[pts]])
                else:
                    copy_on(ev_eng, PTs[pts][:, :, 0, 0:nq], ptv[:, :, 0, 0:nq], reads=[RB[ptb]], writes=[ptr_[pts]])
                    copy_on(ev_eng, PTs[pts][0:64, :, 1, 0:nq], ptv[0:64, :, 1, 0:nq], reads=[RB[ptb]], wadd=[ptr_[pts]])

            def emit_pv(n):
                qi, i = steps[n]
                q = qtiles[qi]
                nq, kw = q["nq"], q["kw"]
                sl = n % 3
                ob = OB
                g = i // 2
                ob_bank = ob[i // 4]
                fns = []
                for hf in range(2):
                    for jh in range(2):
                        jw = min(128, kw - 128 * jh)
                        if q["kind"] == "p":
                            vsrc = Vt[0:jw, q["t"] - 1 + jh, g * 64:(g + 1) * 64]
                        else:
                            vsrc = (Vc if jh == 0 else Vn)[0:jw, q["bb"], g * 64:(g + 1) * 64]
                        out = banks[ob_bank][64 * hf:64 * hf + 64, (i % 4) * 128 + q["oc0"]:(i % 4) * 128 + q["oc0"] + nq]
                        rhs = PTs[n % 2][0:jw, hf, jh, 0:nq]
                        if hf == 0:
                            fns.append(lambda out=out, vsrc=vsrc, rhs=rhs, jh=jh: nc.tensor.matmul(out, lhsT=vsrc, rhs=rhs, start=(jh == 0), stop=(jh == 1)))
                        else:
                            fns.append(lambda out=out, vsrc=vsrc, rhs=rhs, jh=jh: nc.tensor.matmul(out, lhsT=vsrc, rhs=rhs, start=(jh == 0), stop=(jh == 1),
                                                                                                 tile_position=(0, 64)))
                rd = [ptr_[n % 2]] + ([vres[q["t"] - 1], vres[q["t"]]] if q["kind"] == "p" else [vsres])
                pe.group(fns, reads=rd, wadd=[RB[ob_bank]])
                last = (i == 7) and (q["kind"] == "p" or q["bb"] == 1)
                if last:
                    t0 = q["oi"]
                    for half in range(2):
                        eng = act if half == 0 else dve
                        copy_on(eng, attT[:, 4 * half:4 * half + 4, 128 * t0:128 * t0 + 128],
                                banks[ob[half]][:, :].rearrange("p (c q) -> p c q", c=4), reads=[RB[ob[half]]], wadd=[ares[t0]])

            if 'attn' not in SKIP:
                for n in range(NS + 3):
                    if n < NS:
                        emit_qk(n)
                        emit_sm1(n)
                        emit_sm2(n)
                    if 0 <= n - 1 < NS:
                        emit_sm3(n - 1)
                    if 0 <= n - 2 < NS:
                        emit_tr(n - 2)
                    if 0 <= n - 3 < NS:
                        emit_pv(n - 3)
            if debug and s_ == 0:
                dbg("attT", attT, ares)
            if upto <= 3:
                break

            STAGE_MARKS.append((f"{s_}:S4a", pe.cnt, act.cnt, dve.cnt, pool.cnt))
            G = cv_.v(PA_G, [8, NQ], BF16)
            gres = [mk(tag + f"G{t}") for t in range(9)]
            nb_c = [(0, 512), (512, 1024), (1024, 1152)]
            for gi in range(2):
                slot, wr = wget()
                v = wview(slot, 8, 512)

                def evac_ga(fc, bi, n0, n1, psrc, pres, gi=gi):
                    c = 4 * gi + fc
                    act.op(lambda: nc.scalar.activation(out=G[:, c, n0 - 128:n1 - 128], in_=psrc, func=AF.Sigmoid),
                           reads=[pres], wadd=[gres[t - 1] for t in tiles_of(n0, n1)])
                mm_a(range(4), nb_q, lambda kc, fc: v[:, kc, fc * 128:(fc + 1) * 128], lambda kc, n0, n1: hT[:, kc, n0:n1],
                     [wr] + hres, evac_ga)
                wdone()
            STAGE_MARKS.append((f"{s_}:S3", pe.cnt, act.cnt, dve.cnt, pool.cnt))
            for gi in range(2):
                slot, wr = wget()
                v = wview(slot, 8, 512)

                def evac_ao(fc, bi, n0, n1, psrc, pres, gi=gi):
                    c = 4 * gi + fc
                    gr = [gres[t] for t in tiles_of(n0, n1)]
                    dve.op(lambda: nc.vector.tensor_tensor(out=G[:, c, n0:n1], in0=psrc, in1=G[:, c, n0:n1], op=ALU.mult),
                           reads=[pres] + gr, wadd=gr)
                mm_a(range(4), nb_c, lambda kc, fc: v[:, kc, fc * 128:(fc + 1) * 128], lambda kc, n0, n1: attT[:, kc, n0:n1],
                     [wr] + ares, evac_ao)
                wdone()
            if debug and s_ == 0:
                dbg("mixa", G, gres)
            if upto <= 4:
                break

            STAGE_MARKS.append((f"{s_}:S5", pe.cnt, act.cnt, dve.cnt, pool.cnt))
            ZW = 1344
            zT = cv_.v(PA_R1, [8, ZW], BF16)
            uT = cv_.v(PA_R1 + 21504, [8, NQ], BF16)
            sgb = [cv_.v(PA_R1 + 39936 + 2048 * i, [512], F32) for i in range(2)]
            ysqb = [cv_.v(PA_R1 + 44032 + 1024 * i, [512], BF16) for i in range(4)]
            tmpb = [cv_.v(PA_R1 + 49152 + 1024 * i, [512], BF16) for i in range(2)]
            ccb = ccb_t
            ztok = [cv_.v(PA_ATT + 4096 * i, [D], F32) for i in range(3)]
            zres = [mk(tag + f"zT{c}") for c in range(8)]
            ures = [mk(tag + f"uT{c}") for c in range(8)]
            sgr = [mk(tag + f"sg{i}") for i in range(2)]
            tmpr = [mk(tag + f"tmp{i}") for i in range(2)]
            ztres = [mk(tag + f"ztok{i}") for i in range(3)]
            for bb in range(2):
                b = next_bank()
                pb = bankbf(b)
                pe.group([lambda c=c, bb=bb, pb=pb: nc.tensor.transpose(out=pb[:, c * 32:c * 32 + 30], in_=ccb[0:30, bb, c * 128:(c + 1) * 128],
                                                                         identity=identb[0:30, 0:30]) for c in range(8)],
                         reads=[ccres, res("identb")], writes=[RB[b]])
                z0 = 1152 + 96 * bb + 2
                copy_on(alt_eng(), zT[:, :, z0:z0 + 30], pb[:, 0:256].rearrange("p (c j) -> p c j", c=8)[:, :, 0:30], reads=[RB[b]], wadd=zres)
            ztiles = [(9, 0)] + ([(8, 1)] if s_ == n_st - 1 else [])
            sgi = [0]
            for gi in range(4):
                slot, wr = wget()
                v = wview(slot, 8, 512)
                for fc in range(2):
                    c = 2 * gi + fc
                    for (n0, n1) in nb_all:
                        ba = next_bank()
                        bg = next_bank()
                        for (bk, c0) in ((ba, fc * 128), (bg, 256 + fc * 128)):
                            pe.group([lambda kc=kc, bk=bk, c0=c0, n0=n0, n1=n1: nc.tensor.matmul(
                                banks[bk][:, 0:n1 - n0], lhsT=v[:, kc, c0:c0 + 128], rhs=hT[:, kc, n0:n1],
                                start=(kc == 0), stop=(kc == 7)) for kc in range(8)],
                                reads=[wr] + [hres[t] for t in tiles_of(n0, n1)], writes=[RB[bk]])
                        si = sgi[0] % 2
                        sgi[0] += 1
                        w_ = n1 - n0
                        act.op(lambda bg=bg, si=si, w_=w_: nc.scalar.activation(out=sgb[si][:, 0:w_], in_=banks[bg][:, 0:w_], func=AF.Sigmoid),
                               reads=[RB[bg]], writes=[sgr[si]])
                        if n1 < TC:
                            pieces = [(0, w_, n0)]
                        else:
                            pieces = [(0, 128, 1024), (128, 192, 1184), (192, 256, 1280)]
                        for (a0, a1, zc) in pieces:
                            dve.op(lambda ba=ba, si=si, a0=a0, a1=a1, zc=zc, c=c: nc.vector.tensor_tensor(
                                out=zT[:, c, zc:zc + a1 - a0], in0=banks[ba][:, a0:a1], in1=sgb[si][:, a0:a1], op=ALU.mult),
                                reads=[RB[ba], sgr[si]], wadd=[zres[c]])
                    if s_ == 0:
                        dve.op(lambda c=c: nc.vector.tensor_scalar(out=zT[:, c, 98:128], in0=zT[:, c, 98:128], scalar1=flg[:, 1:2], scalar2=None, op0=ALU.mult),
                               reads=[res("flg"), zres[c]], wadd=[zres[c]])
                wdone()
            dg31 = [cv_.v(PA_R1 + 39936 + 4096 * h_, [16, 128], BF16) for h_ in range(2)]
            dg31r = [mk(tag + f"dg31_{h_}") for h_ in range(2)]

            def build_diag(c, h_):
                k0, k1 = (0, 16) if h_ == 0 else (16, 31)
                nk = k1 - k0
                dve.op(lambda: nc.vector.tensor_tensor(
                    out=dg31[h_][:, 0:nk, :], in0=identb[:, :].unsqueeze(1).to_broadcast([128, nk, 128]),
                    in1=vecT[:, c, 11 + k0:11 + k1].unsqueeze(2).to_broadcast([128, nk, 128]), op=ALU.mult),
                    reads=[res("identb"), res("vecT")], writes=[dg31r[h_]])

            build_diag(0, 0)
            build_diag(0, 1)
            zouts = [(1152 + 64, ncs[sbl[0]], 0), (1152 + 96 + 64, ncs[sbl[1]], 1)]
            if s_ == n_st - 1:
                zouts.append((1120, ncp, 2))
            for zi_, (z0, dst, sti) in enumerate(zouts):
                b = next_bank()
                pb = bankbf(b)
                pe.group([lambda c=c, z0=z0, pb=pb: nc.tensor.transpose(out=pb[0:32, c * 128:(c + 1) * 128], in_=zT[:, c, z0:z0 + 32],
                                                                         identity=identb[:, :]) for c in range(8)],
                         reads=zres + [res("identb")], writes=[RB[b]])
                copy_on(act, ztok[sti][0:32, :], pb[0:32, :], reads=[RB[b]], writes=[ztres[sti]])
                out_evs.append(sp.dma([(dst, ztok[sti][2:32, :])], reads=[ztres[sti]]))
            if debug and s_ == 0:
                dbg("zT", zT, zres)
            if upto <= 5:
                break

            STAGE_MARKS.append((f"{s_}:S6", pe.cnt, act.cnt, dve.cnt, pool.cnt))
            cblocks = [(0, 512, 98, 0), (512, 1024, 610, 512), (0, 160, 1154, None)]
            for c in range(8):
                bs = next_bankset()
                for h_ in range(2):
                    k0, k1 = (0, 16) if h_ == 0 else (16, 31)
                    fns = [lambda bi=bi, o0=o0, o1=o1, z0=z0, k=k, k0=k0, h_=h_, c=c, bs=bs: nc.tensor.matmul(
                        banks[bs[bi]][:, 0:o1 - o0], lhsT=dg31[h_][:, k - k0, :], rhs=zT[:, c, z0 + k:z0 + k + o1 - o0],
                        start=(k == 0), stop=(k == 30)) for k in range(k0, k1) for bi, (o0, o1, z0, u0) in enumerate(cblocks)]
                    if h_ == 0:
                        pe.group(fns, reads=[dg31r[h_], zres[c]], writes=[RB[b_] for b_ in bs])
                    else:
                        pe.group(fns, reads=[dg31r[h_], zres[c]], wadd=[RB[b_] for b_ in bs])
                if c + 1 < 8:
                    build_diag(c + 1, 0)
                    build_diag(c + 1, 1)
                if s_ == 0:
                    mb = [0]

                    def bank67():
                        mb[0] += 1
                        return 6 + (mb[0] % 2)
                    item = [(mod_rows, 0, 0), (mod_rows, 0, 1), (mod_scalars, 3, 0), (mod_scalars, 3, 1),
                            (mod_scalars, 4, 0), (mod_scalars, 4, 1), (mod_rows, 1, 0), (mod_rows, 1, 1)][c]
                    if item[0] is mod_rows:
                        mod_rows(item[1], gis=(item[2],), bank_fn=bank67)
                    else:
                        mod_scalars(item[1], {3: 3, 4: 2}[item[1]], gis=(item[2],), bank_fn=bank67)
                for bi, (o0, o1, z0, u0) in enumerate(cblocks):
                    pieces = [(0, o1 - o0, u0)] if u0 is not None else [(0, 64, 1024), (96, 160, 1088)]
                    for (a0, a1, uc) in pieces:
                        src = banks[bs[bi]][:, a0:a1]
                        dst = uT[:, c, uc:uc + a1 - a0]
                        act.op(lambda src=src, dst=dst, c=c: nc.scalar.activation(out=dst, in_=src, func=AF.Identity, bias=vecT[:, c, 2:3], scale=1.0),
                               reads=[RB[bs[bi]], res("vecT")], wadd=[ures[c]])
            if debug and s_ == 0:
                dbg("yconv", uT, ures)
            STAGE_MARKS.append((f"{s_}:S6stats", pe.cnt, act.cnt, dve.cnt, pool.cnt))
            ysqr = [mk(tag + f"ysq{i}") for i in range(4)]
            yi = [0]
            for c in range(8):
                for bi, (n0, n1) in enumerate(nb_c):
                    w_ = n1 - n0
                    kw_ = dict(reads=[ures[c], res("onesb")])
                    if c == 0:
                        kw_["writes"] = [RB[bi]]
                    else:
                        kw_["wadd"] = [RB[bi]]
                    pe.group([lambda bi=bi, c=c, n0=n0, n1=n1, w_=w_: nc.tensor.matmul(banks[bi][:, 0:w_], lhsT=onesb[:, :], rhs=uT[:, c, n0:n1],
                                                                                      start=(c == 0), stop=(c == 7))], **kw_)
            for c in range(8):
                for bi, (n0, n1) in enumerate(nb_c):
                    w_ = n1 - n0
                    qi_ = yi[0] % 4
                    yi[0] += 1
                    act.op(lambda qi_=qi_, c=c, n0=n0, n1=n1, w_=w_: nc.scalar.activation(out=ysqb[qi_][:, 0:w_], in_=uT[:, c, n0:n1], func=AF.Square),
                           reads=[ures[c]], writes=[ysqr[qi_]])
                    kw2 = dict(reads=[ysqr[qi_], res("onesb")])
                    if c == 0:
                        kw2["writes"] = [RB[3 + bi]]
                    else:
                        kw2["wadd"] = [RB[3 + bi]]
                    pe.group([lambda bi=bi, qi_=qi_, w_=w_, c=c: nc.tensor.matmul(banks[3 + bi][:, 0:w_], lhsT=onesb[:, :], rhs=ysqb[qi_][:, 0:w_],
                                                                                 start=(c == 0), stop=(c == 7))], **kw2)
            mu = cv_.v(PA_R1, [NQ], F32)
            rstd = cv_.v(PA_R1 + 4736, [NQ], F32)
            tsl = [cv_.v(PA_R1 + 9472 + 2048 * i, [512], F32) for i in range(3)]
            mur = mk(tag + "mu")
            rstdr = mk(tag + "rstd")
            tslr = [mk(tag + f"tsl{i}") for i in range(3)]
            blk = [(bi, n0, n1, n1 - n0) for bi, (n0, n1) in enumerate(nb_c)]
            for (bi, n0, n1, w_) in blk:
                act.op(lambda bi=bi, n0=n0, n1=n1, w_=w_: nc.scalar.activation(out=mu[:, n0:n1], in_=banks[bi][:, 0:w_], func=AF.Copy, scale=1.0 / D),
                       reads=[RB[bi]], wadd=[mur])
            for (bi, n0, n1, w_) in blk:
                dve.op(lambda n0=n0, n1=n1: nc.vector.tensor_tensor(out=rstd[:, n0:n1], in0=mu[:, n0:n1], in1=mu[:, n0:n1], op=ALU.mult),
                       reads=[mur], wadd=[rstdr])
                dve.op(lambda bi=bi, n0=n0, n1=n1, w_=w_: nc.vector.scalar_tensor_tensor(out=rstd[:, n0:n1], in0=banks[3 + bi][:, 0:w_], scalar=1.0 / D,
                                                                                         in1=rstd[:, n0:n1], op0=ALU.mult, op1=ALU.subtract),
                       reads=[RB[3 + bi], rstdr], wadd=[rstdr])
            for (bi, n0, n1, w_) in blk:
                act.op(lambda n0=n0, n1=n1: nc.scalar.activation(out=rstd[:, n0:n1], in_=rstd[:, n0:n1], func=AF.Sqrt, bias=epsT[:, 0:1], scale=1.0),
                       reads=[rstdr, res("epsT")], wadd=[rstdr])
            for (bi, n0, n1, w_) in blk:
                dve.op(lambda n0=n0, n1=n1: nc.vector.reciprocal(out=rstd[:, n0:n1], in_=rstd[:, n0:n1]), reads=[rstdr], wadd=[rstdr])

            GB = cv_.v(PA_ATT, [8, NQ], BF16)
            gbres = [mk(tag + f"GB{t}") for t in range(9)]

            def gate_b_group(gi, after_fc):
                slot, wr = wget()
                v = wview(slot, 8, 512)

                def evac_gb(fc, bi, n0, n1, psrc, pres, gi=gi):
                    c = 4 * gi + fc
                    act.op(lambda: nc.scalar.activation(out=GB[:, c, n0 - 128:n1 - 128], in_=psrc, func=AF.Sigmoid),
                           reads=[pres], wadd=[gbres[t - 1] for t in tiles_of(n0, n1)])
                for fc in range(4):
                    mm_a([fc], nb_q, lambda kc, fc: v[:, kc, fc * 128:(fc + 1) * 128], lambda kc, n0, n1: hT[:, kc, n0:n1],
                         [wr] + hres, evac_gb)
                    after_fc(4 * gi + fc)
                wdone()

            def ln_apply(c):
                for (n0, n1) in nb_c:
                    w_ = n1 - n0
                    i_ = ti[0] % 3
                    ti[0] += 1
                    dve.op(lambda: nc.vector.tensor_tensor(out=tsl[i_][:, 0:w_], in0=uT[:, c, n0:n1], in1=mu[:, n0:n1], op=ALU.subtract),
                           reads=[ures[c], mur], writes=[tslr[i_]])
                    dve.op(lambda: nc.vector.tensor_tensor(out=tsl[i_][:, 0:w_], in0=tsl[i_][:, 0:w_], in1=rstd[:, n0:n1], op=ALU.mult),
                           reads=[tslr[i_], rstdr], writes=[tslr[i_]])
                    act.op(lambda: nc.scalar.activation(out=uT[:, c, n0:n1], in_=tsl[i_][:, 0:w_], func=AF.Silu,
                                                        bias=vecT[:, c, 4:5], scale=vecT[:, c, 3:4]),
                           reads=[tslr[i_], res("vecT"), ures[c]], wadd=[ures[c]])

            STAGE_MARKS.append((f"{s_}:S6ln", pe.cnt, act.cnt, dve.cnt, pool.cnt))
            ti = [0]
            gate_b_group(0, ln_apply)
            gate_b_group(1, ln_apply)
            if debug and s_ == 0:
                dbg("uT", uT, ures)
            if upto <= 6:
                break

            STAGE_MARKS.append((f"{s_}:S7", pe.cnt, act.cnt, dve.cnt, pool.cnt))
            tmi = [0]
            for gi in range(2):
                slot, wr = wget()
                v = wview(slot, 8, 512)

                def evac_co(fc, bi, n0, n1, psrc, pres, gi=gi):
                    c = 4 * gi + fc
                    i_ = tmi[0] % 2
                    tmi[0] += 1
                    w_ = n1 - n0
                    tl = tiles_of(n0, n1)
                    dve.op(lambda: nc.vector.tensor_tensor(out=tmpb[i_][:, 0:w_], in0=psrc, in1=GB[:, c, n0:n1], op=ALU.mult),
                           reads=[pres] + [gbres[t] for t in tl], writes=[tmpr[i_]])
                    gr = [gres[t] for t in tl]
                    pool.op(lambda: nc.gpsimd.tensor_tensor(out=G[:, c, n0:n1], in0=G[:, c, n0:n1], in1=tmpb[i_][:, 0:w_], op=ALU.add),
                            reads=[tmpr[i_]] + gr, wadd=gr)
                mm_a(range(4), nb_c, lambda kc, fc: v[:, kc, fc * 128:(fc + 1) * 128], lambda kc, n0, n1: uT[:, kc, n0:n1],
                     [wr] + ures, evac_co, first_kc_reads=((lambda kc, wr=wr: [wr, ures[kc]]) if gi == 0 else None))
                wdone()
            if debug and s_ == 0:
                dbg("mix", G, gres)
            if upto <= 7:
                break


            STAGE_MARKS.append((f"{s_}:S9", pe.cnt, act.cnt, dve.cnt, pool.cnt))
            X1 = cv_.v(0, [9, D], F32)
            x1res = [mk(tag + f"X1_{t}") for t in range(9)]
            tmpf = [cv_.v(36864 + 2048 * i, [512], F32) for i in range(2)]
            tmpfr = [mk(tag + f"tmpf{i}") for i in range(2)]
            h2res = [Res(tag + f"h2T{t}").inherit(*hres) for t in range(9)]
            xnb2 = [cv_.v(40960 + 2048 * i, [D], BF16) for i in range(4)]
            sqb2 = cv_.v(49152, [D], BF16)
            R["xn0"] = mk(tag + "xn0b"); R["xn1"] = mk(tag + "xn1b"); R["xn2"] = mk(tag + "xn2b"); R["xn3"] = mk(tag + "xn3b"); R["sqscr"] = mk(tag + "sqscrb")
            for t in range(9):
                if t < 8:
                    src = xp[1024 * s_ + 128 * (t + 1): 1024 * s_ + 128 * (t + 1) + 128, :]
                else:
                    src = xs[128 * s_:128 * s_ + 128, :]
                sp.dma([(X1[:, t, :], src)], writes=[x1res[t]])
            tfi = [0]
            s9b = [0]
            prev_fin = None
            pend = []
            pfin = [None]

            def start_norm(tt):
                batches = [(0, 128, 0)] if tt < 8 else [(0, 64, sb_idx[0]), (64, 128, sb_idx[1])]
                pend.append(norm_to_featmajor(X1[:, tt, :], x1res[tt], hT, 128 * tt, 2, 3, batches, h2res[tt], xnb2, sqb2, split=True, nxn=4))
                if len(pend) >= 3:
                    a2_, fin_ = pend[len(pend) - 3]
                    a2_()
                    if pfin[0] is not None:
                        pfin[0]()
                    pfin[0] = fin_
            for gi in range(2):
                slot, wr = wget()
                v = wview(slot, 8, 512)
                for t in range(9):
                    s9b[0] = (s9b[0] + 1) % 6
                    b = s9b[0]
                    pe.group([lambda kc=kc, t=t, b=b: nc.tensor.matmul(banks[b][:, :], lhsT=G[:, kc, 128 * t:128 * t + 128], rhs=v[:, kc, :],
                                                                         start=(kc == 0), stop=(kc == 7)) for kc in range(8)],
                             reads=[wr, gres[t]], writes=[RB[b]])
                    ty = 0 if t < 8 else ty_s
                    i_ = tfi[0] % 2
                    tfi[0] += 1
                    dve.op(lambda b=b, ty=ty, gi=gi, i_=i_: nc.vector.tensor_tensor(out=tmpf[i_][:, :], in0=banks[b][:, :],
                                                                                    in1=gtrow[:, ty * 2 + 0, 512 * gi:512 * gi + 512], op=ALU.mult),
                           reads=[RB[b], res("gtrow")], writes=[tmpfr[i_]])
                    pool.op(lambda t=t, gi=gi, i_=i_: nc.gpsimd.tensor_tensor(out=X1[:, t, 512 * gi:512 * gi + 512], in0=X1[:, t, 512 * gi:512 * gi + 512],
                                                                              in1=tmpf[i_][:, :], op=ALU.add),
                            reads=[tmpfr[i_], x1res[t]], wadd=[x1res[t]])
                    if gi == 1 and t >= 1:
                        start_norm(t - 1)
                wdone()
            start_norm(8)
            for (a2_, fin_) in pend[-2:]:
                a2_()
                pfin[0]()
                pfin[0] = fin_
            pfin[0]()
            if debug and s_ == 0:
                dbg("X1", X1, x1res)
            if upto <= 10:
                break

            STAGE_MARKS.append((f"{s_}:S11", pe.cnt, act.cnt, dve.cnt, pool.cnt))
            actT = cv_.v(36864, [22, NQ], BF16)
            actres = [mk(tag + f"act{j}") for j in range(22)]
            sgs = [cv_.v(87552 + 1024 * i, [512], BF16) for i in range(2)]
            sgsr = [mk(tag + f"sgs{i}") for i in range(2)]
            sgi = [0]
            for gi in range(11):
                slot, wr = wget()
                v = wview(slot, 8, 512)
                for fc in range(2):
                    j = 2 * gi + fc
                    for (n0, n1) in nb_c:
                        bgt_ = next_bank()
                        bup = next_bank()
                        for (bk, c0) in ((bgt_, fc * 128), (bup, 256 + fc * 128)):
                            pe.group([lambda kc=kc, bk=bk, c0=c0, n0=n0, n1=n1: nc.tensor.matmul(
                                banks[bk][:, 0:n1 - n0], lhsT=v[:, kc, c0:c0 + 128], rhs=hT[:, kc, n0:n1],
                                start=(kc == 0), stop=(kc == 7)) for kc in range(8)],
                                reads=[wr] + [h2res[t] for t in tiles_of(n0, n1)], writes=[RB[bk]])
                        si = sgi[0] % 2
                        sgi[0] += 1
                        w_ = n1 - n0
                        act.op(lambda bgt_=bgt_, si=si, w_=w_: nc.scalar.activation(out=sgs[si][:, 0:w_], in_=banks[bgt_][:, 0:w_], func=AF.Silu),
                               reads=[RB[bgt_]], writes=[sgsr[si]])
                        dve.op(lambda bup=bup, si=si, w_=w_, j=j, n0=n0, n1=n1: nc.vector.tensor_tensor(
                            out=actT[:, j, n0:n1], in0=banks[bup][:, 0:w_], in1=sgs[si][:, 0:w_], op=ALU.mult),
                            reads=[RB[bup], sgsr[si]], wadd=[actres[j]])
                wdone()
            if upto <= 11:
                break


            STAGE_MARKS.append((f"{s_}:S12", pe.cnt, act.cnt, dve.cnt, pool.cnt))
            tmpg = [cv_.v(87552 + 1024 * i, [512], BF16) for i in range(2)]
            tmpgr = [mk(tag + f"tmpg{i}") for i in range(2)]
            tgi = [0]
            if s_ == 0:
                ysr = [Res("ystage0").inherit(res("bgtrep")), Res("ystage1").inherit(res("bgtrep")), Res("ystage2"), Res("ystage3")]

            junkr = Res(tag + "junk").inherit(*h2res)

            def final_norm(t):
                k = nctr[0] % 16
                nctr[0] += 1
                ss = stat[:, 0, k:k + 1]
                sd = stat[:, 1, k:k + 1]
                rs_ = stat[:, 2, k:k + 1]
                rstat = res(f"stat{k}")
                act.op(lambda: nc.scalar.activation(out=hT[:, 0, 0:1024], in_=X1[:, t, :], func=AF.Square, accum_out=ss),
                       reads=[x1res[t]], writes=[junkr, rstat])
                act.op(lambda: nc.scalar.activation(out=sd, in_=ss, func=AF.Sqrt, scale=1.0 / D, bias=epsT[:, 0:1]),
                       reads=[res("epsT")], writes=[rstat])
                dve.op(lambda: nc.vector.reciprocal(out=rs_, in_=sd), writes=[rstat])
                yi_ = t % 4
                dve.op(lambda: nc.vector.scalar_tensor_tensor(out=ystage[yi_][:, :], in0=X1[:, t, :], scalar=rs_, in1=fgrow[:, :],
                                                              op0=ALU.mult, op1=ALU.mult),
                       reads=[x1res[t], rstat, res("fgrow")], writes=[ysr[yi_]])
                if t < 8:
                    dst = yp[1024 * s_ + 128 * t:1024 * s_ + 128 * t + 128, :]
                else:
                    dst = ys[128 * s_:128 * s_ + 128, :]
                out_evs.append(sp.dma([(dst, ystage[yi_][:, :])], reads=[ysr[yi_]]))
            s12b = [0]
            for ch in range(2):
                for kh in range(2):
                    slot, wr = wget()
                    v = wview(slot, 11, 512)
                    for t in range(9):
                        s12b[0] = (s12b[0] + 1) % 8
                        b = s12b[0]
                        pe.group([lambda kc=kc, t=t, b=b, kh=kh: nc.tensor.matmul(banks[b][:, :], lhsT=actT[:, 11 * kh + kc, 128 * t:128 * t + 128], rhs=v[:, kc, :],
                                                                                   start=(kc == 0), stop=(kc == 10)) for kc in range(11)],
                                 reads=[wr] + actres[11 * kh:11 * kh + 11], writes=[RB[b]])
                        ty = 0 if t < 8 else ty_s
                        i_ = tgi[0] % 2
                        tgi[0] += 1
                        dve.op(lambda b=b, ty=ty, ch=ch, i_=i_: nc.vector.tensor_tensor(out=tmpg[i_][:, :], in0=banks[b][:, :],
                                                                                        in1=gtrow[:, ty * 2 + 1, 512 * ch:512 * ch + 512], op=ALU.mult),
                               reads=[RB[b], res("gtrow")], writes=[tmpgr[i_]])
                        pool.op(lambda t=t, ch=ch, i_=i_: nc.gpsimd.tensor_tensor(out=X1[:, t, 512 * ch:512 * ch + 512], in0=X1[:, t, 512 * ch:512 * ch + 512],
                                                                                  in1=tmpg[i_][:, :], op=ALU.add),
                                reads=[tmpgr[i_], x1res[t]], wadd=[x1res[t]])
                        if ch == 1 and kh == 1 and t >= 2:
                            final_norm(t - 2)
                    wdone()
            for t in range(7, 9):
                final_norm(t)

            STAGE_MARKS.append((f"{s_}:S13", pe.cnt, act.cnt, dve.cnt, pool.cnt))
            prev_h2res = h2res + [junkr]

        for ev in out_evs:
            sp.wait(ev)
    return nc


def _rel_bucket_np():
    import jax
    import jax.numpy as jnp
    nb = 16
    max_exact = 8
    q = np.arange(64, dtype=np.int32)[:, None]
    j = np.arange(192, dtype=np.int32)[None, :]
    rel = jnp.asarray(j - 128 - q)
    with jax.default_device(jax.devices("cpu")[0]):
        ret = (rel > 0).astype(jnp.int32) * nb
        n = jnp.abs(rel)
        nf = jnp.maximum(n, 1).astype(jnp.float32)
        large = max_exact + (jnp.log(nf / max_exact) / math.log(128 / max_exact) * (nb - max_exact)).astype(jnp.int32)
        large = jnp.minimum(large, nb - 1)
        out = ret + jnp.where(n < max_exact, n, large)
    return np.asarray(out)


def prep_inputs(x_prompt, x_sample, cache_k, cache_v, cache_conv, c_prompt, c_sample, rel_table,
                w_ada, b_ada, norm1_g, norm2_g, w_in, sink, w_attn_out, dw_w, dw_b, conv_ln_g,
                conv_ln_b, w_conv_out, w_out, w_ffn_up, w_ffn_down, final_g):
    f = lambda a: np.ascontiguousarray(np.asarray(a, dtype=np.float32))
    x_prompt, x_sample = f(x_prompt), f(x_sample)
    ck_all = f(cache_k)[0].reshape(32, 128, 256)
    cv_all = f(cache_v)[0].reshape(32, 128, 256)
    cc_all = f(cache_conv)[0]
    c_prompt, c_sample = f(c_prompt), f(c_sample)
    rel_table = f(rel_table)
    b_ada_ = f(b_ada)[0]
    vecs = np.concatenate([f(norm1_g)[0][None], f(norm2_g)[0][None], f(dw_b)[0][None], f(conv_ln_g)[0][None],
                           f(conv_ln_b)[0][None], b_ada_.reshape(6, D), f(dw_w)[0]], axis=0)
    bgt = np.stack([b_ada_[2 * D:3 * D], b_ada_[5 * D:6 * D]], axis=0)
    fg = f(final_g)[None, :]
    bucket = _rel_bucket_np()
    heads = np.array(SLOT_HEAD)
    tbl = rel_table[bucket][:, :, heads]
    tbl = np.transpose(tbl, (0, 2, 1))
    biasg = np.zeros((128, 16, 256), np.float32)
    maskc = np.full((128, 16, 256), NEG, np.float32)
    biasg[0:64, :, 0:192] = tbl
    maskc[0:64, :, 0:192] = 0.0
    biasg[64:128, :, 64:256] = tbl
    maskc[64:128, :, 64:256] = 0.0
    sinkrow = f(sink)[0][heads][None, :]
    ident = np.eye(128, dtype=np.float32)
    shared = dict(vecs=f(vecs), bgt=f(bgt), fg=fg, biasg=biasg.reshape(128, -1), maskc=maskc.reshape(128, -1), sinkrow=f(sinkrow),
                  ident=ident, w_ada=f(w_ada)[0], w_in=f(w_in)[0], w_ao=f(w_attn_out)[0], w_co=f(w_conv_out)[0],
                  w_o=f(w_out)[0], w_up=f(w_ffn_up)[0], w_dn=f(w_ffn_down)[0])
    in_maps = []
    for c in range(8):
        b, half = c // 2, c % 2
        xp = np.zeros((2176, D), np.float32)
        xp[128:] = x_prompt[b, half * 2048:(half + 1) * 2048]
        if half == 1:
            xp[:128] = x_prompt[b, 2048 - 128:2048]
        flags = np.zeros((128, 2), np.float32)
        flags[:, 0] = NEG if half == 0 else 0.0
        flags[:, 1] = 0.0 if half == 0 else 1.0
        m = dict(shared)
        m.update(xp=xp, xs=f(x_sample[4 * c:4 * c + 4].reshape(256, D)), ck=f(ck_all[4 * c:4 * c + 4]),
                 cv=f(cv_all[4 * c:4 * c + 4]), cc=f(cc_all[4 * c:4 * c + 4]),
                 cvec=f(np.concatenate([c_prompt[b][None], c_sample[4 * c:4 * c + 4]], axis=0)), flags=flags)
        in_maps.append(m)
    return in_maps


_NC_CACHE = {}


def kernel(**inputs):
    in_maps = prep_inputs(**inputs)
    if "nc" not in _NC_CACHE:
        _NC_CACHE["nc"] = build_program()
    nc = _NC_CACHE["nc"]
    res = run_bass_kernel_spmd(nc, in_maps, core_ids=list(range(8)))
    r = res.results
    y_prompt = np.zeros((4, 4096, D), np.float32)
    y_sample = np.zeros((32, 64, D), np.float32)
    nkp = np.zeros((1, 4, 128, 4, 64), np.float32)
    nvp = np.zeros((1, 4, 128, 4, 64), np.float32)
    ncp = np.zeros((1, 4, 30, D), np.float32)
    nks = np.zeros((1, 32, 128, 4, 64), np.float32)
    nvs = np.zeros((1, 32, 128, 4, 64), np.float32)
    ncs = np.zeros((1, 32, 30, D), np.float32)
    for c in range(8):
        b, half = c // 2, c % 2
        y_prompt[b, half * 2048:(half + 1) * 2048] = r[c]["yp"]
        y_sample[4 * c:4 * c + 4] = r[c]["ys"].reshape(4, 64, D)
        if half == 1:
            nkp[0, b] = r[c]["nkp"].reshape(128, 4, 64)
            nvp[0, b] = r[c]["nvp"].reshape(128, 4, 64)
            ncp[0, b] = r[c]["ncp"]
        nks[0, 4 * c:4 * c + 4] = r[c]["nks"].reshape(4, 128, 4, 64)
        nvs[0, 4 * c:4 * c + 4] = r[c]["nvs"].reshape(4, 128, 4, 64)
        ncs[0, 4 * c:4 * c + 4] = r[c]["ncs"]
    return (y_prompt, y_sample, nkp, nvp, ncp, nks, nvs, ncs)
```

```python
import math
from contextlib import ExitStack

import numpy as np
import concourse.bass as bass
import concourse.mybir as mybir
from concourse.bass_utils import run_bass_kernel_spmd

F32 = mybir.dt.float32
BF16 = mybir.dt.bfloat16
AF = mybir.ActivationFunctionType
ALU = mybir.AluOpType
AX = mybir.AxisListType

D = 1024
NT = 10
TC = NT * 128
NQ = TC - 128
DFF = 2816
EPS = 1e-6
NEG = -1e30
SAME_ENG_SYNC = True
import os
SKIP = set(os.environ.get('KSKIP', '').split(','))
STAGE_MARKS = []
NJUNK = int(os.environ.get('NJUNK', '0'))
BIAS_LO = os.environ.get('BIAS_LO', '1') == '1'


class Res:
    def __init__(self, name, excl=False):
        self.name = name
        self.w = {}
        self.r = {}
        self.sem = None
        self.dcnt = 0
        self.excl = excl

    def inherit(self, *olds):
        for o in olds:
            for k, ev in list(o.w.items()) + list(o.r.items()):
                if k not in self.r or self.r[k][1] < ev[1]:
                    self.r[k] = ev
        return self


class Eng:
    def __init__(self, fw, h, name, is_pe=False):
        self.fw = fw
        self.h = h
        self.name = name
        self.sem = fw.new_sem("e_" + name)
        self.cnt = 0
        self.seen = {}
        self.is_pe = is_pe

    def wait(self, ev):
        if ev is None:
            return
        sem, val = ev
        k = id(sem)
        if self.seen.get(k, 0) >= val:
            return
        if sem is self.sem and (self.is_pe or not SAME_ENG_SYNC):
            return
        self.h.wait_ge(sem, val)
        self.seen[k] = val

    def deps(self, reads, writes, wadd):
        for r in reads:
            for ev in list(r.w.values()):
                self.wait(ev)
            if r.excl:
                for ev in list(r.r.values()):
                    if ev[0] is not self.sem:
                        self.wait(ev)
        for w in writes:
            for ev in list(w.w.values()):
                self.wait(ev)
            for ev in list(w.r.values()):
                self.wait(ev)
        for w in wadd:
            for ev in list(w.r.values()):
                self.wait(ev)

    def mark(self, ev, reads, writes, wadd):
        k = id(ev[0])
        for r in reads:
            r.r[k] = ev
        for w in writes:
            w.w = {k: ev}
        for w in wadd:
            w.w[k] = ev

    def op(self, fn, reads=(), writes=(), wadd=()):
        self.deps(reads, writes, wadd)
        inst = fn()
        self.cnt += 1
        inst.then_inc(self.sem, 1)
        ev = (self.sem, self.cnt)
        self.mark(ev, reads, writes, wadd)
        return ev

    def group(self, fns, reads=(), writes=(), wadd=()):
        self.deps(reads, writes, wadd)
        inst = None
        for fn in fns:
            inst = fn()
        self.cnt += 1
        inst.then_inc(self.sem, 1)
        ev = (self.sem, self.cnt)
        self.mark(ev, reads, writes, wadd)
        return ev

    def dma(self, pairs, reads=(), writes=(), wadd=(), **kw):
        self.deps(reads, writes, wadd)
        tgt = writes[0] if writes else (wadd[0] if wadd else reads[0])
        if tgt.sem is None:
            tgt.sem = self.fw.new_sem("d_" + tgt.name)
        for (o, i) in pairs:
            inst = self.h.dma_start(out=o, in_=i, **kw)
            tgt.dcnt += 16
            inst.then_inc(tgt.sem, 16)
        ev = (tgt.sem, tgt.dcnt)
        self.mark(ev, reads, writes, wadd)
        return ev


class FW:
    def __init__(self, nc, stack):
        self.nc = nc
        self.stack = stack
        self.nsem = 0

    def new_sem(self, name):
        self.nsem += 1
        return self.stack.enter_context(self.nc.semaphore(name))

    def sb(self, name, shape, dt):
        return self.stack.enter_context(self.nc.sbuf_tensor(name, shape, dt))

    def ps(self, name, shape, dt):
        return self.stack.enter_context(self.nc.psum_tensor(name, shape, dt))


class Carver:
    def __init__(self, t):
        self.t = t

    def v(self, off, shape, dt):
        esz = 2 if dt == BF16 else 4
        n = int(np.prod(shape))
        assert off % 4 == 0
        a = self.t[:, off // 2: off // 2 + n * esz // 2]
        if dt != BF16:
            a = a.bitcast(dt)
        if len(shape) == 2:
            a = a.rearrange("p (a b) -> p a b", a=shape[0])
        elif len(shape) == 3:
            a = a.rearrange("p (a b c) -> p a b c", a=shape[0], b=shape[1])
        elif len(shape) == 4:
            a = a.rearrange("p (a b c d) -> p a b c d", a=shape[0], b=shape[1], c=shape[2])
        return a


HA = [2 * i for i in range(8)]
HB = [2 * i + 1 for i in range(8)]
SLOT_HEAD = [HA[s // 2] if s % 2 == 0 else HB[s // 2] for s in range(16)]


def build_program(debug=None, n_st=2, upto=99):
    nc = bass.Bass("TRN2", target_bir_lowering=False)

    def din(name, shape):
        return nc.dram_tensor(name, list(shape), F32, kind="ExternalInput").ap()

    def dout(name, shape):
        return nc.dram_tensor(name, list(shape), F32, kind="ExternalOutput").ap()

    xp = din("xp", [2176, D])
    xs = din("xs", [256, D])
    ck = din("ck", [4, 128, 256])
    cv = din("cv", [4, 128, 256])
    cc = din("cc", [4, 30, D])
    cvec = din("cvec", [5, D])
    flags = din("flags", [128, 2])
    vecs = din("vecs", [42, D])
    bgt = din("bgt", [2, D])
    fg = din("fg", [1, D])
    biasg = din("biasg", [128, 16 * 256])
    maskc = din("maskc", [128, 16 * 256])
    sinkrow = din("sinkrow", [1, 16])
    ident_d = din("ident", [128, 128])
    w_ada = din("w_ada", [D, 6 * D])
    w_in = din("w_in", [D, 5632])
    w_ao = din("w_ao", [D, D])
    w_co = din("w_co", [D, D])
    w_o = din("w_o", [D, D])
    w_up = din("w_up", [D, 2 * DFF])
    w_dn = din("w_dn", [DFF, D])

    yp = dout("yp", [2048, D])
    ys = dout("ys", [256, D])
    nkp = dout("nkp", [128, 256])
    nvp = dout("nvp", [128, 256])
    ncp = dout("ncp", [30, D])
    nks = dout("nks", [4, 128, 256])
    nvs = dout("nvs", [4, 128, 256])
    ncs = dout("ncs", [4, 30, D])
    dbg_out = {}
    if debug:
        for name, shape in debug.items():
            dbg_out[name] = dout("dbg_" + name, shape)

    with ExitStack() as st:
        fw = FW(nc, st)
        bhi = fw.sb("bhi", [128, 16, 258], BF16)
        blo = fw.sb("blo", [128, 16, 258], BF16)
        hmfull = fw.sb("hmfull", [128, 128], BF16)
        gtrow = fw.sb("gtrow", [128, 6, D], F32)
        fgrow = fw.sb("fgrow", [128, D], F32)
        identb = fw.sb("identb", [128, 128], BF16)
        identf = fw.sb("identf", [128, 128], F32)
        onesb = fw.sb("onesb", [128, 128], BF16)
        vecT = fw.sb("vecT", [128, 8, 42], F32)
        modT = fw.sb("modT", [128, 4, 8, 5], F32)
        crep = fw.sb("crep", [128, 3, 8, 128], BF16)
        scT = fw.sb("scT", [128, 8, 5], BF16)
        sinkrep = fw.sb("sinkrep", [128, 16], F32)
        nsinkrep = fw.sb("nsinkrep", [128, 16], F32)
        flg = fw.sb("flg", [128, 2], F32)
        epsT = fw.sb("epsT", [128, 1], F32)
        stat = fw.sb("stat", [128, 3, 16], F32)
        wsl = [fw.sb(f"wsl{i}", [128, 5632], BF16) for i in range(2)]
        ystage = [fw.sb(f"ystage{i}", [128, D], F32) for i in range(4)]
        hT = fw.sb("hT", [128, 8, TC], BF16)
        ccb_t = fw.sb("ccb_t", [32, 2, D], BF16)
        POOLB = 90112
        poolt = fw.sb("poolt", [128, POOLB // 2], BF16)
        cv_ = Carver(poolt)
        psum_all = fw.ps("psum_all", [128, 4096], F32)
        banks = [psum_all[:, 512 * i:512 * (i + 1)] for i in range(8)]
        RB = [Res(f"bank{i}", excl=True) for i in range(8)]

        st.enter_context(nc.Block())
        pe = Eng(fw, nc.tensor, "pe", is_pe=True)
        act = Eng(fw, nc.scalar, "act")
        dve = Eng(fw, nc.vector, "dve")
        pool = Eng(fw, nc.gpsimd, "pool")
        sp = Eng(fw, nc.sync, "sp")
        out_evs = []

        R = {}

        def res(name):
            if name not in R:
                R[name] = Res(name)
            return R[name]

        def bankbf(i):
            return banks[i].bitcast(BF16)

        def dbg(name, ap, rs):
            if debug and name in debug:
                out_evs.append(pool.dma([(dbg_out[name], ap)], reads=rs))

        wres = [Res("wsl0"), Res("wsl1")]
        wstate = {"n_issued": 0, "n_used": 0, "plan": []}

        def wplan_add(fn):
            wstate["plan"].append(fn)

        def wprefetch():
            i = wstate["n_issued"]
            if i >= len(wstate["plan"]):
                return
            if i - wstate["n_used"] >= 2:
                return
            s = i % 2
            pairs = wstate["plan"][i](wsl[s])
            pool.dma(pairs, writes=[wres[s]])
            wstate["n_issued"] += 1

        def wget():
            i = wstate["n_used"]
            while wstate["n_issued"] <= i:
                wprefetch()
            s = i % 2
            wstate["n_used"] += 1
            return wsl[s], wres[s]

        def wdone():
            wprefetch()
            wprefetch()

        def wview(slot, kc_n, cols):
            return slot[:, 0:kc_n * cols].rearrange("p (k c) -> p k c", k=kc_n)

        def plain_group(wd, c0, ncols=512):
            def f(slot):
                return [(wview(slot, 8, ncols), wd[:, c0:c0 + ncols].rearrange("(k p) c -> p k c", p=128))]
            return f

        def pair_group(wd, c0, c1):
            def f(slot):
                v = wview(slot, 8, 512)
                return [(v[:, :, 0:256], wd[:, c0:c0 + 256].rearrange("(k p) c -> p k c", p=128)),
                        (v[:, :, 256:512], wd[:, c1:c1 + 256].rearrange("(k p) c -> p k c", p=128))]
            return f

        def q_group(h):
            def f(slot):
                v = slot[:, 0:4096].rearrange("p (k il hf d) -> p k il hf d", k=8, il=4, hf=2)
                prs = []
                for hf in range(2):
                    for il in range(4):
                        c0 = h * 512 + hf * 256 + il * 64
                        src = w_in[:, c0:c0 + 64].rearrange("(k p) d -> p k d", p=128)
                        prs.append((v[:, :, il, hf, :], src))
                return prs
            return f

        def ao_group(g):
            def f(slot):
                v = slot[:, 0:4096].rearrange("p (kh kl c) -> p kh kl c", kh=2, kl=4)
                prs = []
                for hf in range(2):
                    for kh in range(2):
                        r0 = kh * 512 + hf * 256
                        src = w_ao[r0:r0 + 256, g * 512:(g + 1) * 512].rearrange("(kl p) c -> p kl c", p=64)
                        prs.append((v[hf * 64:(hf + 1) * 64, kh, :, :], src))
                return prs
            return f

        def dn_group(ch, kh):
            def f(slot):
                return [(wview(slot, 11, 512), w_dn[kh * 1408:(kh + 1) * 1408, ch * 512:(ch + 1) * 512].rearrange("(k p) c -> p k c", p=128))]
            return f

        bankset_ctr = [0]

        def next_bankset():
            bankset_ctr[0] += 1
            return (0, 1, 2) if bankset_ctr[0] % 2 else (3, 4, 5)

        def mm_a(fcs, nblocks, lhsT_fn, rhs_fn, reads, evac_fn, kn=8, first_kc_reads=None, first_nb_reads=None):
            for idx, fc in enumerate(fcs):
                bs = next_bankset()
                rbs = [RB[bs[bi]] for bi in range(len(nblocks))]

                def mm(kc, bi, n0, n1, fc=fc, bs=bs):
                    return lambda: nc.tensor.matmul(banks[bs[bi]][:, 0:n1 - n0], lhsT=lhsT_fn(kc, fc), rhs=rhs_fn(kc, n0, n1),
                                                    start=(kc == 0), stop=(kc == kn - 1))
                if idx == 0 and first_kc_reads is not None:
                    for kc in range(kn):
                        fns = [mm(kc, bi, n0, n1) for bi, (n0, n1) in enumerate(nblocks)]
                        if kc == 0:
                            pe.group(fns, reads=first_kc_reads(kc), writes=rbs)
                        else:
                            pe.group(fns, reads=first_kc_reads(kc), wadd=rbs)
                elif idx == 0 and first_nb_reads is not None:
                    for bi, (n0, n1) in enumerate(nblocks):
                        fns = [mm(kc, bi, n0, n1) for kc in range(kn)]
                        pe.group(fns, reads=first_nb_reads(bi), writes=[rbs[bi]])
                else:
                    fns = []
                    for kc in range(kn):
                        for bi, (n0, n1) in enumerate(nblocks):
                            fns.append(mm(kc, bi, n0, n1))
                    pe.group(fns, reads=reads, writes=rbs)
                for bi, (n0, n1) in enumerate(nblocks):
                    evac_fn(fc, bi, n0, n1, banks[bs[bi]][:, 0:n1 - n0], RB[bs[bi]])

        rot = {"b": 0, "e": 0}

        def next_bank():
            rot["b"] = (rot["b"] + 1) % 8
            return rot["b"]

        def alt_eng():
            rot["e"] += 1
            return act if rot["e"] % 2 else dve

        def copy_on(eng, out, in_, reads, writes=(), wadd=(), scale=None):
            if eng is act:
                if scale is None:
                    return act.op(lambda: nc.scalar.activation(out=out, in_=in_, func=AF.Copy), reads=reads, writes=writes, wadd=wadd)
                return act.op(lambda: nc.scalar.activation(out=out, in_=in_, func=AF.Copy, scale=scale), reads=reads, writes=writes, wadd=wadd)
            if scale is None:
                return eng.op(lambda: eng.h.tensor_copy(out=out, in_=in_), reads=reads, writes=writes, wadd=wadd)
            return eng.op(lambda: eng.h.tensor_scalar(out=out, in0=in_, scalar1=scale, scalar2=None, op0=ALU.mult),
                          reads=reads, writes=writes, wadd=wadd)

        PA_ATT, PA_R1, PA_G = 0, 18432, 71680
        pro_vt = cv_.v(PA_R1, [D], F32)
        pro_ct = cv_.v(PA_R1 + 4096, [D], F32)
        pro_cs = cv_.v(PA_R1 + 8192, [D], F32)
        pro_mk = cv_.v(PA_R1 + 20480, [16, 256], F32)
        Rc = res("consts")
        sp.dma([(identf[:, :], ident_d)], writes=[res("identf")])
        pool.dma([(identb[:, :], ident_d)], writes=[res("identb")])
        sp.dma([(pro_vt[0:42, :], vecs)], writes=[res("pro_vt")])
        sp.dma([(pro_ct[0:5, :], cvec)], writes=[res("pro_ct")])
        sp.dma([(flg[:, :], flags)], writes=[res("flg")])
        sp.dma([(sinkrep[:, :], sinkrow.to_broadcast([128, 16]))], writes=[res("sinkrep")])
        sp.dma([(fgrow[:, :], fg.to_broadcast([128, D]))], writes=[res("fgrow")])
        sp.dma([(ystage[0][:, :], bgt[0:1, :].to_broadcast([128, D])), (ystage[1][:, :], bgt[1:2, :].to_broadcast([128, D]))],
               writes=[res("bgtrep")])
        pro_bg = cv_.v(PA_R1 + 36864, [16, 256], F32)
        dve.op(lambda: nc.vector.memset(onesb[:, :], 1.0), writes=[res("onesb")])
        dve.op(lambda: nc.vector.memset(epsT[:, :], EPS), writes=[res("epsT")])
        def build_bias_tables():
            sp.dma([(pro_bg, biasg.rearrange("p (s j) -> p s j", s=16))], writes=[res("pro_bg")])
            sp.dma([(pro_mk, maskc.rearrange("p (s j) -> p s j", s=16))], writes=[res("pro_mk")])
            dve.op(lambda: nc.vector.tensor_tensor(out=pro_bg, in0=pro_bg, in1=pro_mk, op=ALU.add),
                   reads=[res("pro_mk"), res("pro_bg")], writes=[res("pro_bg")])
            dve.op(lambda: nc.vector.memset(bhi[:, :, 256:258], 0.0), wadd=[res("biasm")])
            dve.op(lambda: nc.vector.memset(blo[:, :, 256:258], 0.0), wadd=[res("biasm")])
            dve.op(lambda: nc.vector.tensor_copy(out=bhi[:, :, 0:256], in_=pro_bg), reads=[res("pro_bg"), res("biasm")], wadd=[res("biasm")])
            dve.op(lambda: nc.vector.tensor_tensor(out=pro_mk, in0=pro_bg, in1=bhi[:, :, 0:256], op=ALU.subtract),
                   reads=[res("pro_bg"), res("biasm")], writes=[res("pro_mk")])
            dve.op(lambda: nc.vector.tensor_copy(out=blo[:, :, 0:256], in_=pro_mk), reads=[res("pro_mk")], wadd=[res("biasm")])
            dve.op(lambda: nc.vector.tensor_copy(out=bhi[:, :, 256:257], in_=sinkrep[:, :].unsqueeze(2)), reads=[res("sinkrep"), res("biasm")], wadd=[res("biasm")])
            dve.op(lambda: nc.vector.tensor_tensor(out=nsinkrep[:, :].unsqueeze(2), in0=sinkrep[:, :].unsqueeze(2), in1=bhi[:, :, 256:257], op=ALU.subtract),
                   reads=[res("sinkrep"), res("biasm")], writes=[res("nsinkrep")])
            dve.op(lambda: nc.vector.tensor_copy(out=blo[:, :, 256:257], in_=nsinkrep[:, :].unsqueeze(2)), reads=[res("nsinkrep"), res("biasm")], wadd=[res("biasm")])

        dve.op(lambda: nc.vector.tensor_copy(out=hmfull[:, :], in_=flg[:, 0:1].to_broadcast([128, 128])), reads=[res("flg")], writes=[res("hmrow")])
        pe.group([lambda c=c: nc.tensor.transpose(out=banks[6][:, c * 42:(c + 1) * 42], in_=pro_vt[0:42, c * 128:(c + 1) * 128],
                                                   identity=identf[0:42, 0:42]) for c in range(8)],
                 reads=[res("pro_vt"), res("identf")], writes=[RB[6]])
        dve.op(lambda: nc.vector.tensor_copy(out=vecT[:, :, :], in_=banks[6][:, 0:336].rearrange("p (c v) -> p c v", c=8)),
               reads=[RB[6]], writes=[res("vecT")])
        act.op(lambda: nc.scalar.activation(out=pro_cs[0:5, :], in_=pro_ct[0:5, :], func=AF.Silu),
               reads=[res("pro_ct")], writes=[res("pro_cs")])
        pe.group([lambda c=c: nc.tensor.transpose(out=banks[7][:, c * 5:(c + 1) * 5], in_=pro_cs[0:5, c * 128:(c + 1) * 128],
                                                   identity=identf[0:5, 0:5]) for c in range(8)],
                 reads=[res("pro_cs"), res("identf")], writes=[RB[7]])
        dve.op(lambda: nc.vector.tensor_copy(out=scT[:, :, :], in_=banks[7][:, 0:40].rearrange("p (c v) -> p c v", c=8)),
               reads=[RB[7]], writes=[res("scT")])
        for ty, (ba, bb) in enumerate([(0, 0), (1, 2), (3, 4)]):
            dve.op(lambda ty=ty, ba=ba: nc.vector.tensor_copy(out=crep[:, ty, :, 0:64], in_=scT[:, :, ba:ba + 1].to_broadcast([128, 8, 64])),
                   reads=[res("scT")], wadd=[res("crep")])
            dve.op(lambda ty=ty, bb=bb: nc.vector.tensor_copy(out=crep[:, ty, :, 64:128], in_=scT[:, :, bb:bb + 1].to_broadcast([128, 8, 64])),
                   reads=[res("scT")], wadd=[res("crep")])

        def mod_scalars(q, which, gis=(0, 1), bank_fn=None):
            gvec = 0 if q < 3 else 1
            for gi in gis:
                slot, wr = wget()
                v = wview(slot, 8, 512)
                for fc in range(4):
                    c = gi * 4 + fc
                    b = (bank_fn or next_bank)()
                    pe.group([lambda kc=kc, fc=fc, b=b: nc.tensor.matmul(
                        banks[b][:, 0:5], lhsT=v[:, kc, fc * 128:(fc + 1) * 128], rhs=scT[:, kc, :],
                        start=(kc == 0), stop=(kc == 7)) for kc in range(8)],
                        reads=[wr, res("scT")], writes=[RB[b]])
                    dst = modT[:, which, c, :]
                    if q in (0, 3):
                        dve.op(lambda b=b, dst=dst, c=c: nc.vector.tensor_scalar(
                            out=dst, in0=banks[b][:, 0:5], scalar1=vecT[:, c, 5 + q:6 + q], scalar2=None, op0=ALU.add),
                            reads=[RB[b], res("vecT")], wadd=[res("modT")])
                    else:
                        dve.op(lambda b=b, dst=dst, c=c: nc.vector.tensor_scalar(
                            out=dst, in0=banks[b][:, 0:5], scalar1=vecT[:, c, 5 + q:6 + q], scalar2=1.0, op0=ALU.add, op1=ALU.add),
                            reads=[RB[b], res("vecT")], wadd=[res("modT")])
                        dve.op(lambda dst=dst, c=c: nc.vector.tensor_scalar(
                            out=dst, in0=dst, scalar1=vecT[:, c, gvec:gvec + 1], scalar2=None, op0=ALU.mult),
                            reads=[res("vecT"), res("modT")], wadd=[res("modT")])
                wdone()

        def mod_rows(which, gis=(0, 1), bank_fn=None):
            for gi in gis:
                slot, wr = wget()
                v = wview(slot, 8, 512)
                for ty in range(3):
                    b = (bank_fn or next_bank)()
                    pe.group([lambda kc=kc, ty=ty, b=b: nc.tensor.matmul(
                        banks[b][:, :], lhsT=crep[:, ty, kc, :], rhs=v[:, kc, :],
                        start=(kc == 0), stop=(kc == 7)) for kc in range(8)],
                        reads=[wr, res("crep")], writes=[RB[b]])
                    dve.op(lambda b=b, ty=ty, gi=gi: nc.vector.tensor_tensor(
                        out=gtrow[:, ty * 2 + which, gi * 512:(gi + 1) * 512], in0=banks[b][:, :],
                        in1=ystage[which][:, gi * 512:(gi + 1) * 512], op=ALU.add),
                        reads=[RB[b], res("bgtrep")], wadd=[res("gtrow")])
                wdone()

        for s_ in range(n_st):
            if s_ == 0:
                for q in (0, 1):
                    wplan_add(plain_group(w_ada, q * 1024))
                    wplan_add(plain_group(w_ada, q * 1024 + 512))
            wplan_add(plain_group(w_in, 0)); wplan_add(plain_group(w_in, 512)); wplan_add(plain_group(w_in, 1024))
            wplan_add(plain_group(w_in, 3584)); wplan_add(plain_group(w_in, 4096))
            wplan_add(plain_group(w_ao, 0)); wplan_add(plain_group(w_ao, 512))
            for i in range(4):
                wplan_add(pair_group(w_in, 1536 + 256 * i, 2560 + 256 * i))
            if s_ == 0:
                for c0 in (2048, 2560, 3072, 3584, 4096, 4608, 5120, 5632):
                    wplan_add(plain_group(w_ada, c0))
            wplan_add(plain_group(w_in, 4608)); wplan_add(plain_group(w_in, 5120))
            wplan_add(plain_group(w_co, 0)); wplan_add(plain_group(w_co, 512))
            wplan_add(plain_group(w_o, 0)); wplan_add(plain_group(w_o, 512))
            for i in range(11):
                wplan_add(pair_group(w_up, 256 * i, DFF + 256 * i))
            for ch in range(2):
                for kh in range(2):
                    wplan_add(dn_group(ch, kh))

        wprefetch()
        wprefetch()
        mod_scalars(0, 1)
        mod_scalars(1, 0)

        nctr = [0]

        def norm_to_featmajor(x_ap, x_res, dstT, col0, Awhich, Bwhich, batches, dst_res, xn_bufs, sq_buf, split=False, nxn=2, s1mode=False):
            k = nctr[0] % 16
            nctr[0] += 1
            ss = stat[:, 0, k:k + 1]
            sd = stat[:, 1, k:k + 1]
            rs_ = stat[:, 2, k:k + 1]
            xn = xn_bufs[k % nxn]
            xnr = res(f"xn{k % nxn}")
            sqr = res("sqscr")
            rstat = res(f"stat{k}")
            act.op(lambda: nc.scalar.activation(out=sq_buf, in_=x_ap, func=AF.Square, accum_out=ss),
                   reads=[x_res], writes=[sqr, rstat])
            act.op(lambda: nc.scalar.activation(out=sd, in_=ss, func=AF.Sqrt, scale=1.0 / D, bias=epsT[:, 0:1]),
                   reads=[res("epsT")], writes=[rstat])
            dve.op(lambda: nc.vector.reciprocal(out=rs_, in_=sd), writes=[rstat])
            if s1mode:
                act.op(lambda: nc.scalar.activation(out=xn, in_=x_ap, func=AF.Copy, scale=rs_),
                       reads=[x_res, rstat], writes=[xnr])
            else:
                dve.op(lambda: nc.vector.tensor_scalar(out=xn, in0=x_ap, scalar1=rs_, scalar2=None, op0=ALU.mult),
                       reads=[x_res, rstat], writes=[xnr])
            b = (nctr[0] % 2) + 6
            pb = bankbf(b)

            def a2():
                pe.group([lambda c=c: nc.tensor.transpose(out=pb[:, c * 128:(c + 1) * 128], in_=xn[:, c * 128:(c + 1) * 128],
                                                           identity=identb[:, :]) for c in range(8)],
                         reads=[xnr, res("identb")], writes=[RB[b]])
            if not split:
                a2()
            eng = dve if s1mode else alt_eng()

            def fin():
              for c in range(8):
                for (o0, o1, bi) in batches:
                    src = pb[:, c * 128 + o0:c * 128 + o1]
                    dst = dstT[:, c, col0 + o0:col0 + o1]
                    A = modT[:, Awhich, c, bi:bi + 1]
                    B = modT[:, Bwhich, c, bi:bi + 1]
                    if eng is act:
                        act.op(lambda src=src, dst=dst, A=A, B=B: nc.scalar.activation(out=dst, in_=src, func=AF.Identity, bias=B, scale=A),
                               reads=[RB[b], res("modT")], wadd=[dst_res])
                    else:
                        dve.op(lambda src=src, dst=dst, A=A, B=B: nc.vector.tensor_scalar(out=dst, in0=src, scalar1=A, scalar2=B, op0=ALU.mult, op1=ALU.add),
                               reads=[RB[b], res("modT")], wadd=[dst_res])
            return (a2, fin) if split else fin

        carved = [res("pro_vt"), res("pro_ct"), res("pro_cs"), res("pro_mk"), res("pro_bg")]

        def mk(name):
            r_ = Res(name).inherit(*carved)
            carved.append(r_)
            return r_

        for s_ in range(n_st):
            tag = f"s{s_}_"
            sb_idx = (1 + 2 * s_, 2 + 2 * s_)
            sbl = (2 * s_, 2 * s_ + 1)
            ty_s = 1 + s_
            hres = [Res(tag + f"hT{t}") for t in range(NT)]
            if s_ > 0:
                for t in range(NT):
                    hres[t].inherit(*prev_h2res)
            STAGE_MARKS.append((f"{s_}:S1", pe.cnt, act.cnt, dve.cnt, pool.cnt))
            xsl = [cv_.v(PA_R1 + 4096 * i, [D], F32) for i in range(3)]
            xslr = [mk(tag + f"xsl{i}") for i in range(3)]
            xnb = [cv_.v(PA_R1 + 12288 + 2048 * i, [D], BF16) for i in range(2)]
            sqb = cv_.v(PA_R1 + 16384, [D], BF16)
            R["xn0"] = mk(tag + "xn0"); R["xn1"] = mk(tag + "xn1"); R["sqscr"] = mk(tag + "sqscr")
            for t in range(NT):
                sl = t % 3
                if t < 9:
                    src = xp[1024 * s_ + 128 * t: 1024 * s_ + 128 * t + 128, :]
                    batches = [(0, 128, 0)]
                else:
                    src = xs[128 * s_:128 * s_ + 128, :]
                    batches = [(0, 64, sb_idx[0]), (64, 128, sb_idx[1])]
                sp.dma([(xsl[sl], src)], writes=[xslr[sl]])
                fin_ = norm_to_featmajor(xsl[sl], xslr[sl], hT, 128 * t, 0, 1, batches, hres[t], xnb, sqb, s1mode=True)
                if t > 0:
                    prev_fin()
                prev_fin = fin_
            prev_fin()
            if s_ == 0:
                build_bias_tables()
            if debug and s_ == 0:
                dbg("hT", hT[:, :, :], hres)
            if upto <= 1:
                break

            STAGE_MARKS.append((f"{s_}:S2a", pe.cnt, act.cnt, dve.cnt, pool.cnt))
            qT = cv_.v(PA_R1, [8, NQ], BF16)
            kT = cv_.v(PA_R1 + 18432, [2, TC], BF16)
            kTB = cv_.v(PA_R1 + 23552, [2, TC], BF16)
            kTs = cv_.v(PA_R1 + 28672, [2, 2, 192], BF16)
            kTsB = cv_.v(PA_R1 + 30208, [2, 2, 192], BF16)
            Vt = cv_.v(PA_R1 + 31744, [NT, 256], BF16)
            Vc = cv_.v(PA_R1 + 36864, [2, 256], BF16)
            Vn = cv_.v(PA_R1 + 37888, [2, 256], BF16)
            kcb = cv_.v(PA_R1 + 38912, [2, 256], BF16)
            tokst = [cv_.v(PA_R1 + 39936 + 1024 * i, [256], F32) for i in range(1)]
            SW = 257
            spb = [cv_.v(PA_G + 2056 * i, [2, SW], F32) for i in range(3)]
            pbf = [cv_.v(PA_G + 6168 + 2056 * i, [2, SW], F32) for i in range(3)]
            pnb = [cv_.v(PA_G + 12336 + 1024 * i, [2, 256], BF16) for i in range(3)]
            PTs = [cv_.v(PA_G + 15408 + 1024 * i, [2, 2, 128], BF16) for i in range(2)]
            attT = cv_.v(PA_ATT, [8, NQ], BF16)
            qres = [mk(tag + f"qT{t}") for t in range(9)]
            kres = [mk(tag + f"kT{t}") for t in range(NT)]
            vres = [mk(tag + f"V{t}") for t in range(NT)]
            ksres = mk(tag + "kTs")
            vsres = mk(tag + "Vs")
            kcres = mk(tag + "kcb")
            tokres = mk(tag + "tokst")
            kbres = mk(tag + "kTB")
            ksbres = mk(tag + "kTsB")
            spr = [mk(tag + f"sp{i}") for i in range(3)]
            pbr = [mk(tag + f"pb{i}") for i in range(3)]
            pnr = [mk(tag + f"pn{i}") for i in range(3)]
            ptr_ = [mk(tag + f"PTs{i}") for i in range(2)]
            ares = [mk(tag + f"attT{t}") for t in range(9)]
            smA = [mk(tag + f"smA{i}") for i in range(3)]
            smB = [mk(tag + f"smB{i}") for i in range(3)]
            smC = [mk(tag + f"smC{i}") for i in range(3)]
            smst = cv_.v(PA_R1 + 40960, [3, 12], F32)

            nb_q = [(128, 640), (640, 1152), (1152, 1280)]
            nb_all = [(0, 512), (512, 1024), (1024, 1280)]

            def tiles_of(n0, n1):
                return list(range(n0 // 128, (n1 + 127) // 128))

            ccres = res("ccb_t")
            pool.dma([(ccb_t[0:30, bb, :], cc[sbl[bb]]) for bb in range(2)], writes=[ccres])
            pool.dma([(kcb[:, bb, :], ck[sbl[bb]]) for bb in range(2)], writes=[kcres])
            pool.dma([(Vc[:, bb, :], cv[sbl[bb]]) for bb in range(2)], wadd=[vsres])
            for h in range(2):
                slot, wr = wget()
                v = wview(slot, 8, 512)

                def evac_q(fc, bi, n0, n1, psrc, pres, h=h):
                    c = 4 * h + fc
                    eng = alt_eng()
                    copy_on(eng, qT[:, c, n0 - 128:n1 - 128], psrc, reads=[pres], wadd=[qres[t - 1] for t in tiles_of(n0, n1)], scale=0.125)
                mm_a(range(4), nb_q, lambda kc, fc: v[:, kc, fc * 128:(fc + 1) * 128],
                     lambda kc, n0, n1: hT[:, kc, n0:n1], [wr] + hres, evac_q,
                     first_nb_reads=((lambda bi, wr=wr: [wr] + [hres[t] for t in tiles_of(*nb_q[bi])]) if h == 0 else None))
                wdone()
            if 'kv' in SKIP:
                break
            slot, wr = wget()
            v = wview(slot, 8, 512)

            def evac_k(fc, bi, n0, n1, psrc, pres):
                eng = alt_eng()
                copy_on(eng, kT[:, fc, n0:n1], psrc, reads=[pres], wadd=[kres[t] for t in tiles_of(n0, n1)])
            if 'kmm' not in SKIP:
                mm_a(range(2), nb_all, lambda kc, fc: v[:, kc, fc * 128:(fc + 1) * 128], lambda kc, n0, n1: hT[:, kc, n0:n1],
                     [wr] + hres, evac_k)
            if 'kts' not in SKIP:
                for bb in range(2):
                    copy_on(alt_eng(), kTs[:, :, bb, 128:192], kT[:, :, 1152 + 64 * bb:1216 + 64 * bb], reads=[kres[9]], wadd=[ksres])
            sp.dma([(kTB[0:64], kT[64:128]), (kTB[64:128], kT[0:64])], reads=kres, writes=[kbres])
            for t in (range(NT) if 'vmm' not in SKIP else []):
                if t < 9:
                    b = next_bank()
                    pe.group([lambda kc=kc, t=t, b=b: nc.tensor.matmul(banks[b][:, 0:256], lhsT=hT[:, kc, 128 * t:128 * t + 128],
                                                                         rhs=v[:, kc, 256:512], start=(kc == 0), stop=(kc == 7)) for kc in range(8)],
                             reads=[wr, hres[t]], writes=[RB[b]])
                    copy_on(alt_eng(), Vt[:, t, :], banks[b][:, 0:256], reads=[RB[b]], writes=[vres[t]])
                    if t == 8 and s_ == n_st - 1 and 'tok8' not in SKIP:
                        dve.op(lambda b=b: nc.vector.tensor_copy(out=tokst[0], in_=banks[b][:, 0:256]), reads=[RB[b]], writes=[tokres])
                        out_evs.append(sp.dma([(nvp, tokst[0])], reads=[tokres]))
                        b2 = next_bank()
                        pe.group([lambda kc=kc, t=t, b2=b2: nc.tensor.matmul(banks[b2][:, 0:256], lhsT=hT[:, kc, 128 * t:128 * t + 128],
                                                                               rhs=v[:, kc, 0:256], start=(kc == 0), stop=(kc == 7)) for kc in range(8)],
                                 reads=[wr, hres[t]], writes=[RB[b2]])
                        dve.op(lambda b2=b2: nc.vector.tensor_copy(out=tokst[0], in_=banks[b2][:, 0:256]), reads=[RB[b2]], writes=[tokres])
                        out_evs.append(sp.dma([(nkp, tokst[0])], reads=[tokres]))
                elif 'tok9' not in SKIP:
                    for bb in range(2):
                        for which in range(2):
                            b = next_bank()
                            c0 = 256 if which == 0 else 0
                            pe.group([lambda kc=kc, b=b, bb=bb, c0=c0: nc.tensor.matmul(
                                banks[b][0:64, 0:256], lhsT=hT[:, kc, 128 * 9 + 64 * bb:128 * 9 + 64 * bb + 64],
                                rhs=v[:, kc, c0:c0 + 256], start=(kc == 0), stop=(kc == 7)) for kc in range(8)],
                                reads=[wr, hres[9]], writes=[RB[b]])
                            if which == 0:
                                copy_on(act, Vn[0:64, bb, :], banks[b][0:64, 0:256], reads=[RB[b]], wadd=[vsres])
                            dve.op(lambda b=b: nc.vector.tensor_copy(out=tokst[0][0:64, :], in_=banks[b][0:64, 0:256]), reads=[RB[b]], writes=[tokres])
                            dst = (nvs if which == 0 else nks)[sbl[bb], 64:128, :]
                            out_evs.append(sp.dma([(dst, tokst[0][0:64, :])], reads=[tokres]))
            wdone()
            if s_ == 0 and 'cc' not in SKIP:
                rcp = res("cache_copy")
                out_evs.append(sp.dma([(nks[:, 0:64, :], ck[:, 64:128, :]), (nvs[:, 0:64, :], cv[:, 64:128, :])], reads=[rcp]))
            for bb in (range(2) if 'kcb' not in SKIP else []):
                b = next_bank()
                pb = bankbf(b)
                pe.group([lambda c=c, bb=bb, pb=pb: nc.tensor.transpose(out=pb[:, c * 128:(c + 1) * 128], in_=kcb[:, bb, c * 128:(c + 1) * 128],
                                                                         identity=identb[:, :]) for c in range(2)],
                         reads=[kcres, res("identb")], writes=[RB[b]])
                copy_on(alt_eng(), kTs[:, :, bb, 0:128], pb[:, 0:256].rearrange("p (c j) -> p c j", c=2), reads=[RB[b]], wadd=[ksres])
            if 'swap' not in SKIP:
                sp.dma([(kTsB[0:64], kTs[64:128]), (kTsB[64:128], kTs[0:64])], reads=[ksres], writes=[ksbres])
            if debug and s_ == 0:
                dbg("qT", qT, qres)
                dbg("kT", kT, kres)
                dbg("Vt", Vt, vres)
            if upto <= 2:
                break

            STAGE_MARKS.append((f"{s_}:S2b", pe.cnt, act.cnt, dve.cnt, pool.cnt))
            qtiles = []
            for t in range(1, 9):
                qtiles.append(dict(nq=128, kw=256, qc0=128 * (t - 1), t=t, kind="p", bb=0, oc0=0, oi=t - 1))
            for bb in range(2):
                qtiles.append(dict(nq=64, kw=192, qc0=1024 + 64 * bb, t=9, kind="s", bb=bb, oc0=64 * bb, oi=8))
            steps = [(qi, i) for qi in range(len(qtiles)) for i in range(8)]
            NS = len(steps)
            OB = (6, 7)

            def emit_qk(n):
                qi, i = steps[n]
                q = qtiles[qi]
                sb2 = 2 * (n % 2)
                g = i // 2
                fns = []
                rd = [qres[q["t"] - 1]]
                for hf in range(2):
                    base = 64 * hf
                    lhsT = qT[base:base + 64, i, q["qc0"]:q["qc0"] + q["nq"]]
                    use_b = (g % 2) != hf
                    if q["kind"] == "p":
                        t = q["t"]
                        src = kTB if use_b else kT
                        rhs = src[base:base + 64, g // 2, 128 * t - 128:128 * t + 128]
                        rd += [kbres] if use_b else [kres[t - 1], kres[t]]
                    else:
                        src = kTsB if use_b else kTs
                        rhs = src[base:base + 64, g // 2, q["bb"], :]
                        rd += [ksbres] if use_b else [ksres]
                    nq_, kw_ = q["nq"], q["kw"]
                    h_ = 2 * i + hf
                    bk = banks[sb2 + hf]
                    if kw_ == 256:
                        fns.append(lambda bk=bk, h_=h_: nc.tensor.matmul(bk[:, 0:258], lhsT=identb[:, :], rhs=bhi[:, h_, 0:258], start=True, stop=False))
                        if s_ == 0 and q["t"] == 1:
                            fns.append(lambda bk=bk: nc.tensor.matmul(bk[:, 0:128], lhsT=identb[:, :], rhs=hmfull[:, :], start=False, stop=False))
                        if BIAS_LO:
                            fns.append(lambda bk=bk, h_=h_: nc.tensor.matmul(bk[:, 0:258], lhsT=identb[:, :], rhs=blo[:, h_, 0:258], start=False, stop=False))
                    else:
                        pb_ = 64 * hf
                        c0_ = 64 * hf
                        idn = identb[pb_:pb_ + 64, pb_:pb_ + 64]
                        fns.append(lambda bk=bk, idn=idn, h_=h_, pb_=pb_, c0_=c0_: nc.tensor.matmul(
                            bk[0:64, 0:192], lhsT=idn, rhs=bhi[pb_:pb_ + 64, h_, c0_:c0_ + 192], start=True, stop=False))
                        if BIAS_LO:
                            fns.append(lambda bk=bk, idn=idn, h_=h_, pb_=pb_, c0_=c0_: nc.tensor.matmul(
                                bk[0:64, 0:192], lhsT=idn, rhs=blo[pb_:pb_ + 64, h_, c0_:c0_ + 192], start=False, stop=False))
                        fns.append(lambda bk=bk, idn=idn, h_=h_, pb_=pb_: nc.tensor.matmul(
                            bk[0:64, 192:194], lhsT=idn, rhs=bhi[pb_:pb_ + 64, h_, 256:258], start=False, stop=False))
                        if BIAS_LO:
                            fns.append(lambda bk=bk, idn=idn, h_=h_, pb_=pb_: nc.tensor.matmul(
                                bk[0:64, 192:194], lhsT=idn, rhs=blo[pb_:pb_ + 64, h_, 256:258], start=False, stop=False))
                    fns.append(lambda bk=bk, lhsT=lhsT, rhs=rhs, nq_=nq_, kw_=kw_: nc.tensor.matmul(
                        bk[0:nq_, 0:kw_], lhsT=lhsT, rhs=rhs, start=False, stop=True))
                    for _j in range(NJUNK):
                        fns.append(lambda bk=bk: nc.tensor.matmul(bk[:, 260:510], lhsT=identb[:, :], rhs=hT[:, 0, 0:250], start=False, stop=False))
                rd += [res("biasm"), res("identb"), res("hmrow"), res("onesb")]
                pe.group(fns, reads=rd + hres[0:2], writes=[RB[sb2], RB[sb2 + 1]])

            def emit_sm1(n):
                qi, i = steps[n]
                q = qtiles[qi]
                nq, kw = q["nq"], q["kw"]
                sl = n % 3
                sb2 = 2 * (n % 2)
                mx = smst[0:nq, sl, 0:2]
                negm = smst[0:nq, sl, 2:4]
                for hf in range(2):
                    dve.op(lambda hf=hf: nc.vector.tensor_reduce(out=smst[0:nq, sl, hf:hf + 1], in_=banks[sb2 + hf][0:nq, 0:kw + 1], axis=AX.X, op=ALU.max),
                           reads=[RB[sb2 + hf]], wadd=[smA[sl]])
                dve.op(lambda: nc.vector.tensor_scalar(out=negm, in0=mx, scalar1=-1.0, scalar2=None, op0=ALU.mult),
                       reads=[smA[sl]], writes=[smA[sl]])

            def emit_sm2(n):
                qi, i = steps[n]
                q = qtiles[qi]
                nq, kw = q["nq"], q["kw"]
                sl = n % 3
                sb2 = 2 * (n % 2)
                for hf in range(2):
                    act.op(lambda hf=hf: nc.scalar.activation(out=pbf[sl][0:nq, hf, 0:kw + 1], in_=banks[sb2 + hf][0:nq, 0:kw + 1], func=AF.Exp,
                                                              bias=smst[0:nq, sl, 2 + hf:3 + hf], accum_out=smst[0:nq, sl, 4 + hf:5 + hf]),
                           reads=[RB[sb2 + hf], smA[sl]], wadd=[pbr[sl], smB[sl]])

            def emit_sm3(n):
                qi, i = steps[n]
                q = qtiles[qi]
                nq, kw = q["nq"], q["kw"]
                sl = n % 3
                rinv = smst[0:nq, sl, 10:12]
                dve.op(lambda: nc.vector.reciprocal(out=rinv, in_=smst[0:nq, sl, 4:6]), reads=[smB[sl]], writes=[smC[sl]])
                pool.op(lambda: nc.gpsimd.tensor_tensor(out=pnb[sl][0:nq, :, 0:kw], in0=pbf[sl][0:nq, :, 0:kw],
                                                        in1=rinv.unsqueeze(2).to_broadcast([nq, 2, kw]), op=ALU.mult),
                        reads=[pbr[sl], smC[sl]], writes=[pnr[sl]])

            def emit_tr(n):
                qi, i = steps[n]
                q = qtiles[qi]
                nq, kw = q["nq"], q["kw"]
                sl = n % 3
                ps_ = n % 2
                pts = n % 2
                ptb = 4 + ps_
                ptv = bankbf(ptb)[:, 0:512].rearrange("p (h j q) -> p h j q", h=2, j=2)
                fns = []
                for hf in range(2):
                    for jh in range(2):
                        jw = min(128, kw - 128 * jh)
                        fns.append(lambda hf=hf, jh=jh, jw=jw: nc.tensor.transpose(
                            out=ptv[0:jw, hf, jh, 0:nq], in_=pnb[sl][0:nq, hf, 128 * jh:128 * jh + jw], identity=identb[0:nq, 0:nq]))
                pe.group(fns, reads=[pnr[sl], res("identb")], writes=[RB[ptb]])
                ev_eng = dve
                if kw == 256:
                    copy_on(ev_eng, PTs[pts].rearrange("p h j q -> p (h j q)"), bankbf(ptb)[:, 0:512], reads=[RB[ptb]], writes=[ptr_[pts]])
                else:
                    copy_on(ev_eng, PTs[pts][:, :, 0, 0:nq], ptv[:, :, 0, 0:nq], reads=[RB[ptb]], writes=[ptr_[pts]])
                    copy_on(ev_eng, PTs[pts][0:64, :, 1, 0:nq], ptv[0:64, :, 1, 0:nq], reads=[RB[ptb]], wadd=[ptr_[pts]])

            def emit_pv(n):
                qi, i = steps[n]
                q = qtiles[qi]
                nq, kw = q["nq"], q["kw"]
                sl = n % 3
                ob = OB
                g = i // 2
                ob_bank = ob[i // 4]
                fns = []
                for hf in range(2):
                    for jh in range(2):
                        jw = min(128, kw - 128 * jh)
                        if q["kind"] == "p":
                            vsrc = Vt[0:jw, q["t"] - 1 + jh, g * 64:(g + 1) * 64]
                        else:
                            vsrc = (Vc if jh == 0 else Vn)[0:jw, q["bb"], g * 64:(g + 1) * 64]
                        out = banks[ob_bank][64 * hf:64 * hf + 64, (i % 4) * 128 + q["oc0"]:(i % 4) * 128 + q["oc0"] + nq]
                        rhs = PTs[n % 2][0:jw, hf, jh, 0:nq]
                        if hf == 0:
                            fns.append(lambda out=out, vsrc=vsrc, rhs=rhs, jh=jh: nc.tensor.matmul(out, lhsT=vsrc, rhs=rhs, start=(jh == 0), stop=(jh == 1)))
                        else:
                            fns.append(lambda out=out, vsrc=vsrc, rhs=rhs, jh=jh: nc.tensor.matmul(out, lhsT=vsrc, rhs=rhs, start=(jh == 0), stop=(jh == 1),
                                                                                                 tile_position=(0, 64)))
                rd = [ptr_[n % 2]] + ([vres[q["t"] - 1], vres[q["t"]]] if q["kind"] == "p" else [vsres])
                pe.group(fns, reads=rd, wadd=[RB[ob_bank]])
                last = (i == 7) and (q["kind"] == "p" or q["bb"] == 1)
                if last:
                    t0 = q["oi"]
                    for half in range(2):
                        eng = act if half == 0 else dve
                        copy_on(eng, attT[:, 4 * half:4 * half + 4, 128 * t0:128 * t0 + 128],
                                banks[ob[half]][:, :].rearrange("p (c q) -> p c q", c=4), reads=[RB[ob[half]]], wadd=[ares[t0]])

            if 'attn' not in SKIP:
                for n in range(NS + 3):
                    if n < NS:
                        emit_qk(n)
                        emit_sm1(n)
                        emit_sm2(n)
                    if 0 <= n - 1 < NS:
                        emit_sm3(n - 1)
                    if 0 <= n - 2 < NS:
                        emit_tr(n - 2)
                    if 0 <= n - 3 < NS:
                        emit_pv(n - 3)
            if debug and s_ == 0:
                dbg("attT", attT, ares)
            if upto <= 3:
                break

            STAGE_MARKS.append((f"{s_}:S4a", pe.cnt, act.cnt, dve.cnt, pool.cnt))
            G = cv_.v(PA_G, [8, NQ], BF16)
            gres = [mk(tag + f"G{t}") for t in range(9)]
            nb_c = [(0, 512), (512, 1024), (1024, 1152)]
            for gi in range(2):
                slot, wr = wget()
                v = wview(slot, 8, 512)

                def evac_ga(fc, bi, n0, n1, psrc, pres, gi=gi):
                    c = 4 * gi + fc
                    act.op(lambda: nc.scalar.activation(out=G[:, c, n0 - 128:n1 - 128], in_=psrc, func=AF.Sigmoid),
                           reads=[pres], wadd=[gres[t - 1] for t in tiles_of(n0, n1)])
                mm_a(range(4), nb_q, lambda kc, fc: v[:, kc, fc * 128:(fc + 1) * 128], lambda kc, n0, n1: hT[:, kc, n0:n1],
                     [wr] + hres, evac_ga)
                wdone()
            STAGE_MARKS.append((f"{s_}:S3", pe.cnt, act.cnt, dve.cnt, pool.cnt))
            for gi in range(2):
                slot, wr = wget()
                v = wview(slot, 8, 512)

                def evac_ao(fc, bi, n0, n1, psrc, pres, gi=gi):
                    c = 4 * gi + fc
                    gr = [gres[t] for t in tiles_of(n0, n1)]
                    dve.op(lambda: nc.vector.tensor_tensor(out=G[:, c, n0:n1], in0=psrc, in1=G[:, c, n0:n1], op=ALU.mult),
                           reads=[pres] + gr, wadd=gr)
                mm_a(range(4), nb_c, lambda kc, fc: v[:, kc, fc * 128:(fc + 1) * 128], lambda kc, n0, n1: attT[:, kc, n0:n1],
                     [wr] + ares, evac_ao)
                wdone()
            if debug and s_ == 0:
                dbg("mixa", G, gres)
            if upto <= 4:
                break

            STAGE_MARKS.append((f"{s_}:S5", pe.cnt, act.cnt, dve.cnt, pool.cnt))
            ZW = 1344
            zT = cv_.v(PA_R1, [8, ZW], BF16)
            uT = cv_.v(PA_R1 + 21504, [8, NQ], BF16)
            sgb = [cv_.v(PA_R1 + 39936 + 2048 * i, [512], F32) for i in range(2)]
            ysqb = [cv_.v(PA_R1 + 44032 + 1024 * i, [512], BF16) for i in range(4)]
            tmpb = [cv_.v(PA_R1 + 49152 + 1024 * i, [512], BF16) for i in range(2)]
            ccb = ccb_t
            ztok = [cv_.v(PA_ATT + 4096 * i, [D], F32) for i in range(3)]
            zres = [mk(tag + f"zT{c}") for c in range(8)]
            ures = [mk(tag + f"uT{c}") for c in range(8)]
            sgr = [mk(tag + f"sg{i}") for i in range(2)]
            tmpr = [mk(tag + f"tmp{i}") for i in range(2)]
            ztres = [mk(tag + f"ztok{i}") for i in range(3)]
            for bb in range(2):
                b = next_bank()
                pb = bankbf(b)
                pe.group([lambda c=c, bb=bb, pb=pb: nc.tensor.transpose(out=pb[:, c * 32:c * 32 + 30], in_=ccb[0:30, bb, c * 128:(c + 1) * 128],
                                                                         identity=identb[0:30, 0:30]) for c in range(8)],
                         reads=[ccres, res("identb")], writes=[RB[b]])
                z0 = 1152 + 96 * bb + 2
                copy_on(alt_eng(), zT[:, :, z0:z0 + 30], pb[:, 0:256].rearrange("p (c j) -> p c j", c=8)[:, :, 0:30], reads=[RB[b]], wadd=zres)
            ztiles = [(9, 0)] + ([(8, 1)] if s_ == n_st - 1 else [])
            sgi = [0]
            for gi in range(4):
                slot, wr = wget()
                v = wview(slot, 8, 512)
                for fc in range(2):
                    c = 2 * gi + fc
                    for (n0, n1) in nb_all:
                        ba = next_bank()
                        bg = next_bank()
                        for (bk, c0) in ((ba, fc * 128), (bg, 256 + fc * 128)):
                            pe.group([lambda kc=kc, bk=bk, c0=c0, n0=n0, n1=n1: nc.tensor.matmul(
                                banks[bk][:, 0:n1 - n0], lhsT=v[:, kc, c0:c0 + 128], rhs=hT[:, kc, n0:n1],
                                start=(kc == 0), stop=(kc == 7)) for kc in range(8)],
                                reads=[wr] + [hres[t] for t in tiles_of(n0, n1)], writes=[RB[bk]])
                        si = sgi[0] % 2
                        sgi[0] += 1
                        w_ = n1 - n0
                        act.op(lambda bg=bg, si=si, w_=w_: nc.scalar.activation(out=sgb[si][:, 0:w_], in_=banks[bg][:, 0:w_], func=AF.Sigmoid),
                               reads=[RB[bg]], writes=[sgr[si]])
                        if n1 < TC:
                            pieces = [(0, w_, n0)]
                        else:
                            pieces = [(0, 128, 1024), (128, 192, 1184), (192, 256, 1280)]
                        for (a0, a1, zc) in pieces:
                            dve.op(lambda ba=ba, si=si, a0=a0, a1=a1, zc=zc, c=c: nc.vector.tensor_tensor(
                                out=zT[:, c, zc:zc + a1 - a0], in0=banks[ba][:, a0:a1], in1=sgb[si][:, a0:a1], op=ALU.mult),
                                reads=[RB[ba], sgr[si]], wadd=[zres[c]])
                    if s_ == 0:
                        dve.op(lambda c=c: nc.vector.tensor_scalar(out=zT[:, c, 98:128], in0=zT[:, c, 98:128], scalar1=flg[:, 1:2], scalar2=None, op0=ALU.mult),
                               reads=[res("flg"), zres[c]], wadd=[zres[c]])
                wdone()
            dg31 = [cv_.v(PA_R1 + 39936 + 4096 * h_, [16, 128], BF16) for h_ in range(2)]
            dg31r = [mk(tag + f"dg31_{h_}") for h_ in range(2)]

            def build_diag(c, h_):
                k0, k1 = (0, 16) if h_ == 0 else (16, 31)
                nk = k1 - k0
                dve.op(lambda: nc.vector.tensor_tensor(
                    out=dg31[h_][:, 0:nk, :], in0=identb[:, :].unsqueeze(1).to_broadcast([128, nk, 128]),
                    in1=vecT[:, c, 11 + k0:11 + k1].unsqueeze(2).to_broadcast([128, nk, 128]), op=ALU.mult),
                    reads=[res("identb"), res("vecT")], writes=[dg31r[h_]])

            build_diag(0, 0)
            build_diag(0, 1)
            zouts = [(1152 + 64, ncs[sbl[0]], 0), (1152 + 96 + 64, ncs[sbl[1]], 1)]
            if s_ == n_st - 1:
                zouts.append((1120, ncp, 2))
            for zi_, (z0, dst, sti) in enumerate(zouts):
                b = next_bank()
                pb = bankbf(b)
                pe.group([lambda c=c, z0=z0, pb=pb: nc.tensor.transpose(out=pb[0:32, c * 128:(c + 1) * 128], in_=zT[:, c, z0:z0 + 32],
                                                                         identity=identb[:, :]) for c in range(8)],
                         reads=zres + [res("identb")], writes=[RB[b]])
                copy_on(act, ztok[sti][0:32, :], pb[0:32, :], reads=[RB[b]], writes=[ztres[sti]])
                out_evs.append(sp.dma([(dst, ztok[sti][2:32, :])], reads=[ztres[sti]]))
            if debug and s_ == 0:
                dbg("zT", zT, zres)
            if upto <= 5:
                break

            STAGE_MARKS.append((f"{s_}:S6", pe.cnt, act.cnt, dve.cnt, pool.cnt))
            cblocks = [(0, 512, 98, 0), (512, 1024, 610, 512), (0, 160, 1154, None)]
            for c in range(8):
                bs = next_bankset()
                for h_ in range(2):
                    k0, k1 = (0, 16) if h_ == 0 else (16, 31)
                    fns = [lambda bi=bi, o0=o0, o1=o1, z0=z0, k=k, k0=k0, h_=h_, c=c, bs=bs: nc.tensor.matmul(
                        banks[bs[bi]][:, 0:o1 - o0], lhsT=dg31[h_][:, k - k0, :], rhs=zT[:, c, z0 + k:z0 + k + o1 - o0],
                        start=(k == 0), stop=(k == 30)) for k in range(k0, k1) for bi, (o0, o1, z0, u0) in enumerate(cblocks)]
                    if h_ == 0:
                        pe.group(fns, reads=[dg31r[h_], zres[c]], writes=[RB[b_] for b_ in bs])
                    else:
                        pe.group(fns, reads=[dg31r[h_], zres[c]], wadd=[RB[b_] for b_ in bs])
                if c + 1 < 8:
                    build_diag(c + 1, 0)
                    build_diag(c + 1, 1)
                if s_ == 0:
                    mb = [0]

                    def bank67():
                        mb[0] += 1
                        return 6 + (mb[0] % 2)
                    item = [(mod_rows, 0, 0), (mod_rows, 0, 1), (mod_scalars, 3, 0), (mod_scalars, 3, 1),
                            (mod_scalars, 4, 0), (mod_scalars, 4, 1), (mod_rows, 1, 0), (mod_rows, 1, 1)][c]
                    if item[0] is mod_rows:
                        mod_rows(item[1], gis=(item[2],), bank_fn=bank67)
                    else:
                        mod_scalars(item[1], {3: 3, 4: 2}[item[1]], gis=(item[2],), bank_fn=bank67)
                for bi, (o0, o1, z0, u0) in enumerate(cblocks):
                    pieces = [(0, o1 - o0, u0)] if u0 is not None else [(0, 64, 1024), (96, 160, 1088)]
                    for (a0, a1, uc) in pieces:
                        src = banks[bs[bi]][:, a0:a1]
                        dst = uT[:, c, uc:uc + a1 - a0]
                        act.op(lambda src=src, dst=dst, c=c: nc.scalar.activation(out=dst, in_=src, func=AF.Identity, bias=vecT[:, c, 2:3], scale=1.0),
                               reads=[RB[bs[bi]], res("vecT")], wadd=[ures[c]])
            if debug and s_ == 0:
                dbg("yconv", uT, ures)
            STAGE_MARKS.append((f"{s_}:S6stats", pe.cnt, act.cnt, dve.cnt, pool.cnt))
            ysqr = [mk(tag + f"ysq{i}") for i in range(4)]
            yi = [0]
            for c in range(8):
                for bi, (n0, n1) in enumerate(nb_c):
                    w_ = n1 - n0
                    kw_ = dict(reads=[ures[c], res("onesb")])
                    if c == 0:
                        kw_["writes"] = [RB[bi]]
                    else:
                        kw_["wadd"] = [RB[bi]]
                    pe.group([lambda bi=bi, c=c, n0=n0, n1=n1, w_=w_: nc.tensor.matmul(banks[bi][:, 0:w_], lhsT=onesb[:, :], rhs=uT[:, c, n0:n1],
                                                                                      start=(c == 0), stop=(c == 7))], **kw_)
            for c in range(8):
                for bi, (n0, n1) in enumerate(nb_c):
                    w_ = n1 - n0
                    qi_ = yi[0] % 4
                    yi[0] += 1
                    act.op(lambda qi_=qi_, c=c, n0=n0, n1=n1, w_=w_: nc.scalar.activation(out=ysqb[qi_][:, 0:w_], in_=uT[:, c, n0:n1], func=AF.Square),
                           reads=[ures[c]], writes=[ysqr[qi_]])
                    kw2 = dict(reads=[ysqr[qi_], res("onesb")])
                    if c == 0:
                        kw2["writes"] = [RB[3 + bi]]
                    else:
                        kw2["wadd"] = [RB[3 + bi]]
                    pe.group([lambda bi=bi, qi_=qi_, w_=w_, c=c: nc.tensor.matmul(banks[3 + bi][:, 0:w_], lhsT=onesb[:, :], rhs=ysqb[qi_][:, 0:w_],
                                                                                 start=(c == 0), stop=(c == 7))], **kw2)
            mu = cv_.v(PA_R1, [NQ], F32)
            rstd = cv_.v(PA_R1 + 4736, [NQ], F32)
            tsl = [cv_.v(PA_R1 + 9472 + 2048 * i, [512], F32) for i in range(3)]
            mur = mk(tag + "mu")
            rstdr = mk(tag + "rstd")
            tslr = [mk(tag + f"tsl{i}") for i in range(3)]
            blk = [(bi, n0, n1, n1 - n0) for bi, (n0, n1) in enumerate(nb_c)]
            for (bi, n0, n1, w_) in blk:
                act.op(lambda bi=bi, n0=n0, n1=n1, w_=w_: nc.scalar.activation(out=mu[:, n0:n1], in_=banks[bi][:, 0:w_], func=AF.Copy, scale=1.0 / D),
                       reads=[RB[bi]], wadd=[mur])
            for (bi, n0, n1, w_) in blk:
                dve.op(lambda n0=n0, n1=n1: nc.vector.tensor_tensor(out=rstd[:, n0:n1], in0=mu[:, n0:n1], in1=mu[:, n0:n1], op=ALU.mult),
                       reads=[mur], wadd=[rstdr])
                dve.op(lambda bi=bi, n0=n0, n1=n1, w_=w_: nc.vector.scalar_tensor_tensor(out=rstd[:, n0:n1], in0=banks[3 + bi][:, 0:w_], scalar=1.0 / D,
                                                                                         in1=rstd[:, n0:n1], op0=ALU.mult, op1=ALU.subtract),
                       reads=[RB[3 + bi], rstdr], wadd=[rstdr])
            for (bi, n0, n1, w_) in blk:
                act.op(lambda n0=n0, n1=n1: nc.scalar.activation(out=rstd[:, n0:n1], in_=rstd[:, n0:n1], func=AF.Sqrt, bias=epsT[:, 0:1], scale=1.0),
                       reads=[rstdr, res("epsT")], wadd=[rstdr])
            for (bi, n0, n1, w_) in blk:
                dve.op(lambda n0=n0, n1=n1: nc.vector.reciprocal(out=rstd[:, n0:n1], in_=rstd[:, n0:n1]), reads=[rstdr], wadd=[rstdr])

            GB = cv_.v(PA_ATT, [8, NQ], BF16)
            gbres = [mk(tag + f"GB{t}") for t in range(9)]

            def gate_b_group(gi, after_fc):
                slot, wr = wget()
                v = wview(slot, 8, 512)

                def evac_gb(fc, bi, n0, n1, psrc, pres, gi=gi):
                    c = 4 * gi + fc
                    act.op(lambda: nc.scalar.activation(out=GB[:, c, n0 - 128:n1 - 128], in_=psrc, func=AF.Sigmoid),
                           reads=[pres], wadd=[gbres[t - 1] for t in tiles_of(n0, n1)])
                for fc in range(4):
                    mm_a([fc], nb_q, lambda kc, fc: v[:, kc, fc * 128:(fc + 1) * 128], lambda kc, n0, n1: hT[:, kc, n0:n1],
                         [wr] + hres, evac_gb)
                    after_fc(4 * gi + fc)
                wdone()

            def ln_apply(c):
                for (n0, n1) in nb_c:
                    w_ = n1 - n0
                    i_ = ti[0] % 3
                    ti[0] += 1
                    dve.op(lambda: nc.vector.tensor_tensor(out=tsl[i_][:, 0:w_], in0=uT[:, c, n0:n1], in1=mu[:, n0:n1], op=ALU.subtract),
                           reads=[ures[c], mur], writes=[tslr[i_]])
                    dve.op(lambda: nc.vector.tensor_tensor(out=tsl[i_][:, 0:w_], in0=tsl[i_][:, 0:w_], in1=rstd[:, n0:n1], op=ALU.mult),
                           reads=[tslr[i_], rstdr], writes=[tslr[i_]])
                    act.op(lambda: nc.scalar.activation(out=uT[:, c, n0:n1], in_=tsl[i_][:, 0:w_], func=AF.Silu,
                                                        bias=vecT[:, c, 4:5], scale=vecT[:, c, 3:4]),
                           reads=[tslr[i_], res("vecT"), ures[c]], wadd=[ures[c]])

            STAGE_MARKS.append((f"{s_}:S6ln", pe.cnt, act.cnt, dve.cnt, pool.cnt))
            ti = [0]
            gate_b_group(0, ln_apply)
            gate_b_group(1, ln_apply)
            if debug and s_ == 0:
                dbg("uT", uT, ures)
            if upto <= 6:
                break

            STAGE_MARKS.append((f"{s_}:S7", pe.cnt, act.cnt, dve.cnt, pool.cnt))
            tmi = [0]
            for gi in range(2):
                slot, wr = wget()
                v = wview(slot, 8, 512)

                def evac_co(fc, bi, n0, n1, psrc, pres, gi=gi):
                    c = 4 * gi + fc
                    i_ = tmi[0] % 2
                    tmi[0] += 1
                    w_ = n1 - n0
                    tl = tiles_of(n0, n1)
                    dve.op(lambda: nc.vector.tensor_tensor(out=tmpb[i_][:, 0:w_], in0=psrc, in1=GB[:, c, n0:n1], op=ALU.mult),
                           reads=[pres] + [gbres[t] for t in tl], writes=[tmpr[i_]])
                    gr = [gres[t] for t in tl]
                    pool.op(lambda: nc.gpsimd.tensor_tensor(out=G[:, c, n0:n1], in0=G[:, c, n0:n1], in1=tmpb[i_][:, 0:w_], op=ALU.add),
                            reads=[tmpr[i_]] + gr, wadd=gr)
                mm_a(range(4), nb_c, lambda kc, fc: v[:, kc, fc * 128:(fc + 1) * 128], lambda kc, n0, n1: uT[:, kc, n0:n1],
                     [wr] + ures, evac_co, first_kc_reads=((lambda kc, wr=wr: [wr, ures[kc]]) if gi == 0 else None))
                wdone()
            if debug and s_ == 0:
                dbg("mix", G, gres)
            if upto <= 7:
                break


            STAGE_MARKS.append((f"{s_}:S9", pe.cnt, act.cnt, dve.cnt, pool.cnt))
            X1 = cv_.v(0, [9, D], F32)
            x1res = [mk(tag + f"X1_{t}") for t in range(9)]
            tmpf = [cv_.v(36864 + 2048 * i, [512], F32) for i in range(2)]
            tmpfr = [mk(tag + f"tmpf{i}") for i in range(2)]
            h2res = [Res(tag + f"h2T{t}").inherit(*hres) for t in range(9)]
            xnb2 = [cv_.v(40960 + 2048 * i, [D], BF16) for i in range(4)]
            sqb2 = cv_.v(49152, [D], BF16)
            R["xn0"] = mk(tag + "xn0b"); R["xn1"] = mk(tag + "xn1b"); R["xn2"] = mk(tag + "xn2b"); R["xn3"] = mk(tag + "xn3b"); R["sqscr"] = mk(tag + "sqscrb")
            for t in range(9):
                if t < 8:
                    src = xp[1024 * s_ + 128 * (t + 1): 1024 * s_ + 128 * (t + 1) + 128, :]
                else:
                    src = xs[128 * s_:128 * s_ + 128, :]
                sp.dma([(X1[:, t, :], src)], writes=[x1res[t]])
            tfi = [0]
            s9b = [0]
            prev_fin = None
            pend = []
            pfin = [None]

            def start_norm(tt):
                batches = [(0, 128, 0)] if tt < 8 else [(0, 64, sb_idx[0]), (64, 128, sb_idx[1])]
                pend.append(norm_to_featmajor(X1[:, tt, :], x1res[tt], hT, 128 * tt, 2, 3, batches, h2res[tt], xnb2, sqb2, split=True, nxn=4))
                if len(pend) >= 3:
                    a2_, fin_ = pend[len(pend) - 3]
                    a2_()
                    if pfin[0] is not None:
                        pfin[0]()
                    pfin[0] = fin_
            for gi in range(2):
                slot, wr = wget()
                v = wview(slot, 8, 512)
                for t in range(9):
                    s9b[0] = (s9b[0] + 1) % 6
                    b = s9b[0]
                    pe.group([lambda kc=kc, t=t, b=b: nc.tensor.matmul(banks[b][:, :], lhsT=G[:, kc, 128 * t:128 * t + 128], rhs=v[:, kc, :],
                                                                         start=(kc == 0), stop=(kc == 7)) for kc in range(8)],
                             reads=[wr, gres[t]], writes=[RB[b]])
                    ty = 0 if t < 8 else ty_s
                    i_ = tfi[0] % 2
                    tfi[0] += 1
                    dve.op(lambda b=b, ty=ty, gi=gi, i_=i_: nc.vector.tensor_tensor(out=tmpf[i_][:, :], in0=banks[b][:, :],
                                                                                    in1=gtrow[:, ty * 2 + 0, 512 * gi:512 * gi + 512], op=ALU.mult),
                           reads=[RB[b], res("gtrow")], writes=[tmpfr[i_]])
                    pool.op(lambda t=t, gi=gi, i_=i_: nc.gpsimd.tensor_tensor(out=X1[:, t, 512 * gi:512 * gi + 512], in0=X1[:, t, 512 * gi:512 * gi + 512],
                                                                              in1=tmpf[i_][:, :], op=ALU.add),
                            reads=[tmpfr[i_], x1res[t]], wadd=[x1res[t]])
                    if gi == 1 and t >= 1:
                        start_norm(t - 1)
                wdone()
            start_norm(8)
            for (a2_, fin_) in pend[-2:]:
                a2_()
                pfin[0]()
                pfin[0] = fin_
            pfin[0]()
            if debug and s_ == 0:
                dbg("X1", X1, x1res)
            if upto <= 10:
                break

            STAGE_MARKS.append((f"{s_}:S11", pe.cnt, act.cnt, dve.cnt, pool.cnt))
            actT = cv_.v(36864, [22, NQ], BF16)
            actres = [mk(tag + f"act{j}") for j in range(22)]
            sgs = [cv_.v(87552 + 1024 * i, [512], BF16) for i in range(2)]
            sgsr = [mk(tag + f"sgs{i}") for i in range(2)]
            sgi = [0]
            for gi in range(11):
                slot, wr = wget()
                v = wview(slot, 8, 512)
                for fc in range(2):
                    j = 2 * gi + fc
                    for (n0, n1) in nb_c:
                        bgt_ = next_bank()
                        bup = next_bank()
                        for (bk, c0) in ((bgt_, fc * 128), (bup, 256 + fc * 128)):
                            pe.group([lambda kc=kc, bk=bk, c0=c0, n0=n0, n1=n1: nc.tensor.matmul(
                                banks[bk][:, 0:n1 - n0], lhsT=v[:, kc, c0:c0 + 128], rhs=hT[:, kc, n0:n1],
                                start=(kc == 0), stop=(kc == 7)) for kc in range(8)],
                                reads=[wr] + [h2res[t] for t in tiles_of(n0, n1)], writes=[RB[bk]])
                        si = sgi[0] % 2
                        sgi[0] += 1
                        w_ = n1 - n0
                        act.op(lambda bgt_=bgt_, si=si, w_=w_: nc.scalar.activation(out=sgs[si][:, 0:w_], in_=banks[bgt_][:, 0:w_], func=AF.Silu),
                               reads=[RB[bgt_]], writes=[sgsr[si]])
                        dve.op(lambda bup=bup, si=si, w_=w_, j=j, n0=n0, n1=n1: nc.vector.tensor_tensor(
                            out=actT[:, j, n0:n1], in0=banks[bup][:, 0:w_], in1=sgs[si][:, 0:w_], op=ALU.mult),
                            reads=[RB[bup], sgsr[si]], wadd=[actres[j]])
                wdone()
            if upto <= 11:
                break


            STAGE_MARKS.append((f"{s_}:S12", pe.cnt, act.cnt, dve.cnt, pool.cnt))
            tmpg = [cv_.v(87552 + 1024 * i, [512], BF16) for i in range(2)]
            tmpgr = [mk(tag + f"tmpg{i}") for i in range(2)]
            tgi = [0]
            if s_ == 0:
                ysr = [Res("ystage0").inherit(res("bgtrep")), Res("ystage1").inherit(res("bgtrep")), Res("ystage2"), Res("ystage3")]

            junkr = Res(tag + "junk").inherit(*h2res)

            def final_norm(t):
                k = nctr[0] % 16
                nctr[0] += 1
                ss = stat[:, 0, k:k + 1]
                sd = stat[:, 1, k:k + 1]
                rs_ = stat[:, 2, k:k + 1]
                rstat = res(f"stat{k}")
                act.op(lambda: nc.scalar.activation(out=hT[:, 0, 0:1024], in_=X1[:, t, :], func=AF.Square, accum_out=ss),
                       reads=[x1res[t]], writes=[junkr, rstat])
                act.op(lambda: nc.scalar.activation(out=sd, in_=ss, func=AF.Sqrt, scale=1.0 / D, bias=epsT[:, 0:1]),
                       reads=[res("epsT")], writes=[rstat])
                dve.op(lambda: nc.vector.reciprocal(out=rs_, in_=sd), writes=[rstat])
                yi_ = t % 4
                dve.op(lambda: nc.vector.scalar_tensor_tensor(out=ystage[yi_][:, :], in0=X1[:, t, :], scalar=rs_, in1=fgrow[:, :],
                                                              op0=ALU.mult, op1=ALU.mult),
                       reads=[x1res[t], rstat, res("fgrow")], writes=[ysr[yi_]])
                if t < 8:
                    dst = yp[1024 * s_ + 128 * t:1024 * s_ + 128 * t + 128, :]
                else:
                    dst = ys[128 * s_:128 * s_ + 128, :]
                out_evs.append(sp.dma([(dst, ystage[yi_][:, :])], reads=[ysr[yi_]]))
            s12b = [0]
            for ch in range(2):
                for kh in range(2):
                    slot, wr = wget()
                    v = wview(slot, 11, 512)
                    for t in range(9):
                        s12b[0] = (s12b[0] + 1) % 8
                        b = s12b[0]
                        pe.group([lambda kc=kc, t=t, b=b, kh=kh: nc.tensor.matmul(banks[b][:, :], lhsT=actT[:, 11 * kh + kc, 128 * t:128 * t + 128], rhs=v[:, kc, :],
                                                                                   start=(kc == 0), stop=(kc == 10)) for kc in range(11)],
                                 reads=[wr] + actres[11 * kh:11 * kh + 11], writes=[RB[b]])
                        ty = 0 if t < 8 else ty_s
                        i_ = tgi[0] % 2
                        tgi[0] += 1
                        dve.op(lambda b=b, ty=ty, ch=ch, i_=i_: nc.vector.tensor_tensor(out=tmpg[i_][:, :], in0=banks[b][:, :],
                                                                                        in1=gtrow[:, ty * 2 + 1, 512 * ch:512 * ch + 512], op=ALU.mult),
                               reads=[RB[b], res("gtrow")], writes=[tmpgr[i_]])
                        pool.op(lambda t=t, ch=ch, i_=i_: nc.gpsimd.tensor_tensor(out=X1[:, t, 512 * ch:512 * ch + 512], in0=X1[:, t, 512 * ch:512 * ch + 512],
                                                                                  in1=tmpg[i_][:, :], op=ALU.add),
                                reads=[tmpgr[i_], x1res[t]], wadd=[x1res[t]])
                        if ch == 1 and kh == 1 and t >= 2:
                            final_norm(t - 2)
                    wdone()
            for t in range(7, 9):
                final_norm(t)

            STAGE_MARKS.append((f"{s_}:S13", pe.cnt, act.cnt, dve.cnt, pool.cnt))
            prev_h2res = h2res + [junkr]

        for ev in out_evs:
            sp.wait(ev)
    return nc


def _rel_bucket_np():
    import jax
    import jax.numpy as jnp
    nb = 16
    max_exact = 8
    q = np.arange(64, dtype=np.int32)[:, None]
    j = np.arange(192, dtype=np.int32)[None, :]
    rel = jnp.asarray(j - 128 - q)
    with jax.default_device(jax.devices("cpu")[0]):
        ret = (rel > 0).astype(jnp.int32) * nb
        n = jnp.abs(rel)
        nf = jnp.maximum(n, 1).astype(jnp.float32)
        large = max_exact + (jnp.log(nf / max_exact) / math.log(128 / max_exact) * (nb - max_exact)).astype(jnp.int32)
        large = jnp.minimum(large, nb - 1)
        out = ret + jnp.where(n < max_exact, n, large)
    return np.asarray(out)


def prep_inputs(x_prompt, x_sample, cache_k, cache_v, cache_conv, c_prompt, c_sample, rel_table,
                w_ada, b_ada, norm1_g, norm2_g, w_in, sink, w_attn_out, dw_w, dw_b, conv_ln_g,
                conv_ln_b, w_conv_out, w_out, w_ffn_up, w_ffn_down, final_g):
    f = lambda a: np.ascontiguousarray(np.asarray(a, dtype=np.float32))
    x_prompt, x_sample = f(x_prompt), f(x_sample)
    ck_all = f(cache_k)[0].reshape(32, 128, 256)
    cv_all = f(cache_v)[0].reshape(32, 128, 256)
    cc_all = f(cache_conv)[0]
    c_prompt, c_sample = f(c_prompt), f(c_sample)
    rel_table = f(rel_table)
    b_ada_ = f(b_ada)[0]
    vecs = np.concatenate([f(norm1_g)[0][None], f(norm2_g)[0][None], f(dw_b)[0][None], f(conv_ln_g)[0][None],
                           f(conv_ln_b)[0][None], b_ada_.reshape(6, D), f(dw_w)[0]], axis=0)
    bgt = np.stack([b_ada_[2 * D:3 * D], b_ada_[5 * D:6 * D]], axis=0)
    fg = f(final_g)[None, :]
    bucket = _rel_bucket_np()
    heads = np.array(SLOT_HEAD)
    tbl = rel_table[bucket][:, :, heads]
    tbl = np.transpose(tbl, (0, 2, 1))
    biasg = np.zeros((128, 16, 256), np.float32)
    maskc = np.full((128, 16, 256), NEG, np.float32)
    biasg[0:64, :, 0:192] = tbl
    maskc[0:64, :, 0:192] = 0.0
    biasg[64:128, :, 64:256] = tbl
    maskc[64:128, :, 64:256] = 0.0
    sinkrow = f(sink)[0][heads][None, :]
    ident = np.eye(128, dtype=np.float32)
    shared = dict(vecs=f(vecs), bgt=f(bgt), fg=fg, biasg=biasg.reshape(128, -1), maskc=maskc.reshape(128, -1), sinkrow=f(sinkrow),
                  ident=ident, w_ada=f(w_ada)[0], w_in=f(w_in)[0], w_ao=f(w_attn_out)[0], w_co=f(w_conv_out)[0],
                  w_o=f(w_out)[0], w_up=f(w_ffn_up)[0], w_dn=f(w_ffn_down)[0])
    in_maps = []
    for c in range(8):
        b, half = c // 2, c % 2
        xp = np.zeros((2176, D), np.float32)
        xp[128:] = x_prompt[b, half * 2048:(half + 1) * 2048]
        if half == 1:
            xp[:128] = x_prompt[b, 2048 - 128:2048]
        flags = np.zeros((128, 2), np.float32)
        flags[:, 0] = NEG if half == 0 else 0.0
        flags[:, 1] = 0.0 if half == 0 else 1.0
        m = dict(shared)
        m.update(xp=xp, xs=f(x_sample[4 * c:4 * c + 4].reshape(256, D)), ck=f(ck_all[4 * c:4 * c + 4]),
                 cv=f(cv_all[4 * c:4 * c + 4]), cc=f(cc_all[4 * c:4 * c + 4]),
                 cvec=f(np.concatenate([c_prompt[b][None], c_sample[4 * c:4 * c + 4]], axis=0)), flags=flags)
        in_maps.append(m)
    return in_maps


_NC_CACHE = {}


def kernel(**inputs):
    in_maps = prep_inputs(**inputs)
    if "nc" not in _NC_CACHE:
        _NC_CACHE["nc"] = build_program()
    nc = _NC_CACHE["nc"]
    res = run_bass_kernel_spmd(nc, in_maps, core_ids=list(range(8)))
    r = res.results
    y_prompt = np.zeros((4, 4096, D), np.float32)
    y_sample = np.zeros((32, 64, D), np.float32)
    nkp = np.zeros((1, 4, 128, 4, 64), np.float32)
    nvp = np.zeros((1, 4, 128, 4, 64), np.float32)
    ncp = np.zeros((1, 4, 30, D), np.float32)
    nks = np.zeros((1, 32, 128, 4, 64), np.float32)
    nvs = np.zeros((1, 32, 128, 4, 64), np.float32)
    ncs = np.zeros((1, 32, 30, D), np.float32)
    for c in range(8):
        b, half = c // 2, c % 2
        y_prompt[b, half * 2048:(half + 1) * 2048] = r[c]["yp"]
        y_sample[4 * c:4 * c + 4] = r[c]["ys"].reshape(4, 64, D)
        if half == 1:
            nkp[0, b] = r[c]["nkp"].reshape(128, 4, 64)
            nvp[0, b] = r[c]["nvp"].reshape(128, 4, 64)
            ncp[0, b] = r[c]["ncp"]
        nks[0, 4 * c:4 * c + 4] = r[c]["nks"].reshape(4, 128, 4, 64)
        nvs[0, 4 * c:4 * c + 4] = r[c]["nvs"].reshape(4, 128, 4, 64)
        ncs[0, 4 * c:4 * c + 4] = r[c]["ncs"]
    return (y_prompt, y_sample, nkp, nvp, ncp, nks, nvs, ncs)
```
